# Optimizing a Trainium2 kernel written in Bass

```python
import jax, jax.numpy as jnp
from jax import lax
import numpy as np

D_MODEL = 1024
BATCH = 16
SEQ = 4096
DEPTH = 2

GRID_W = 64
CTX_LEN = 256
N_MIXERS = 2
N_NA_LAYERS = (DEPTH + 1) // 2
N_HG_LAYERS = DEPTH // 2
BRANCH = D_MODEL
NA_HEADS = 16
NA_HEAD_DIM = BRANCH // NA_HEADS
NA_WIN_H = 8
NA_WIN_W = 16
NA_QCOL_BLOCK = 16
NA_BAND_W = NA_QCOL_BLOCK + NA_WIN_W - 1
HG_HEADS = 8
HG_HEAD_DIM = BRANCH // HG_HEADS
HG_CHUNK = 64
EPS = 1e-6

kernel_name = "hybrid_natten_hgrn2_diffusion_block"


def rms_norm(x, w):
    xf = x.astype(jnp.float32)
    y = xf * lax.rsqrt(jnp.mean(xf * xf, axis=-1, keepdims=True) + EPS)
    return (y * w.astype(jnp.float32)).astype(x.dtype)


def adaln_params(cond, w, b):
    m = jax.nn.silu(cond) @ w + b
    return jnp.split(m, 3, axis=-1)


def modulate(x, norm_w, shift, scale):
    return rms_norm(x, norm_w) * (1 + scale) + shift


def split_heads(t, n_heads):
    return t.reshape(t.shape[:-1] + (n_heads, t.shape[-1] // n_heads))


def _na_column_tables():
    cols = np.arange(GRID_W)
    col_start = np.clip(cols - NA_WIN_W // 2, 0, GRID_W - NA_WIN_W)
    n_cb = GRID_W // NA_QCOL_BLOCK
    band_start = np.minimum(col_start[np.arange(n_cb) * NA_QCOL_BLOCK], GRID_W - NA_BAND_W)
    band_cols = band_start[:, None] + np.arange(NA_BAND_W)[None, :]
    q_cols = cols.reshape(n_cb, NA_QCOL_BLOCK)
    q_start = col_start.reshape(n_cb, NA_QCOL_BLOCK)
    kcol = band_cols[:, None, :]
    in_window = (kcol >= q_start[..., None]) & (kcol < q_start[..., None] + NA_WIN_W)
    dc_index = np.clip(kcol - q_cols[..., None] + NA_WIN_W - 1, 0, 2 * NA_WIN_W - 2)
    return band_cols, in_window, dc_index


def na_attend(q, k, v, k_ctx, v_ctx, rpb):
    B, L, H, dh = q.shape
    rows = L // GRID_W
    kh = min(NA_WIN_H, rows)
    n_cb = GRID_W // NA_QCOL_BLOCK
    band_cols, in_window, dc_index = _na_column_tables()
    col_mask = jnp.asarray(in_window)[:, :, None, :]
    dc_idx = jnp.asarray(dc_index)[:, :, None, :]
    scale = dh ** -0.5
    qg = q.reshape(B, rows, n_cb, NA_QCOL_BLOCK, H, dh)
    kg = k.reshape(B, rows, GRID_W, H, dh)
    vg = v.reshape(B, rows, GRID_W, H, dh)
    n_loc = kh * NA_BAND_W

    def one_row(r):
        r0 = jnp.clip(r - kh // 2, 0, rows - kh)
        q_r = lax.dynamic_index_in_dim(qg, r, axis=1, keepdims=False)
        k_band = lax.dynamic_slice_in_dim(kg, r0, kh, axis=1)[:, :, band_cols]
        v_band = lax.dynamic_slice_in_dim(vg, r0, kh, axis=1)[:, :, band_cols]
        s_loc = jnp.einsum('bjqhd,bkjwhd->bhjqkw', q_r, k_band).astype(jnp.float32) * scale
        dr_idx = (r0 + jnp.arange(kh) - r + (NA_WIN_H - 1))[None, None, :, None]
        bias = rpb[:, dr_idx, dc_idx].astype(jnp.float32)
        s_loc = jnp.where(col_mask, s_loc + bias[None], -jnp.inf)
        s_loc = s_loc.reshape(B, H, n_cb, NA_QCOL_BLOCK, n_loc)
        s_ctx = jnp.einsum('bjqhd,bmhd->bhjqm', q_r, k_ctx).astype(jnp.float32) * scale
        p = jax.nn.softmax(jnp.concatenate([s_loc, s_ctx], axis=-1), axis=-1).astype(v.dtype)
        p_loc = p[..., :n_loc].reshape(B, H, n_cb, NA_QCOL_BLOCK, kh, NA_BAND_W)
        o = (jnp.einsum('bhjqkw,bkjwhd->bjqhd', p_loc, v_band)
             + jnp.einsum('bhjqm,bmhd->bjqhd', p[..., n_loc:], v_ctx))
        return o.reshape(B, GRID_W, H, dh)

    o = lax.map(one_row, jnp.arange(rows))
    return jnp.moveaxis(o, 0, 1).reshape(B, L, H, dh)


def dense_attention(q, k, v):
    s = jnp.einsum('bqhd,bkhd->bhqk', q, k).astype(jnp.float32) * (q.shape[-1] ** -0.5)
    p = jax.nn.softmax(s, axis=-1).astype(v.dtype)
    return jnp.einsum('bhqk,bkhd->bqhd', p, v)


def na_layer(x, xc, mod, mod_c, norm_w, w_in, rpb, w_out, need_ctx):
    shift, scale, gate = mod
    shift_c, scale_c, gate_c = mod_c
    B, L, _ = x.shape
    h = modulate(x, norm_w, shift, scale)
    q, k, v, z = jnp.split(h @ w_in, 4, axis=-1)
    hc = modulate(xc, norm_w, shift_c, scale_c)
    if need_ctx:
        qc, kc, vc, zc = jnp.split(hc @ w_in, 4, axis=-1)
    else:
        kc, vc = jnp.split(hc @ w_in[:, BRANCH:3 * BRANCH], 2, axis=-1)
    kc_h, vc_h = split_heads(kc, NA_HEADS), split_heads(vc, NA_HEADS)
    o = na_attend(split_heads(q, NA_HEADS), split_heads(k, NA_HEADS), split_heads(v, NA_HEADS), kc_h, vc_h, rpb)
    o = o.reshape(B, L, BRANCH) * jax.nn.silu(z)
    x = x + gate * (o @ w_out)
    if need_ctx:
        oc = dense_attention(split_heads(qc, NA_HEADS), kc_h, vc_h).reshape(xc.shape[:2] + (BRANCH,))
        xc = xc + gate_c * ((oc * jax.nn.silu(zc)) @ w_out)
    return x, xc


def hg_heads(t):
    return jnp.swapaxes(split_heads(t, HG_HEADS), 1, 2)


def hgrn_query(t):
    return hg_heads(jax.nn.silu(t.astype(jnp.float32)) * (HG_HEAD_DIM ** -0.5))


def hgrn_decay(raw, lb):
    raw = raw.astype(jnp.float32)
    log_f = jnp.logaddexp(jnp.log(lb), jnp.log1p(-lb) + jax.nn.log_sigmoid(raw))
    k = (1 - lb) * jax.nn.sigmoid(-raw)
    return hg_heads(log_f), hg_heads(k)


def hgrn_chunk_scan(q, k, v, log_f, s0):
    B, H, T, _ = q.shape
    n = T // HG_CHUNK
    causal = jnp.tril(jnp.ones((HG_CHUNK, HG_CHUNK), dtype=bool))[:, :, None]

    def to_chunks(a):
        return jnp.moveaxis(a.reshape(B, H, n, HG_CHUNK, a.shape[-1]), 2, 0)

    def step(S, inp):
        qc, kc, vc, ac = inp
        A = jnp.cumsum(ac, axis=2)
        o_inter = jnp.einsum('bhtk,bhkv->bhtv', qc * jnp.exp(A), S)
        rel = jnp.where(causal, A[:, :, :, None, :] - A[:, :, None, :, :], -jnp.inf)
        scores = jnp.einsum('bhtk,bhsk,bhtsk->bhts', qc, kc, jnp.exp(rel))
        o_intra = jnp.einsum('bhts,bhsv->bhtv', scores, vc)
        A_last = A[:, :, -1:, :]
        S_new = (jnp.exp(A_last[:, :, 0, :])[..., None] * S
                 + jnp.einsum('bhsk,bhsv->bhkv', kc * jnp.exp(A_last - A), vc))
        return S_new, o_inter + o_intra

    S_fin, o = lax.scan(step, s0, (to_chunks(q), to_chunks(k), to_chunks(v), to_chunks(log_f)))
    return jnp.moveaxis(o, 0, 2).reshape(B, H, T, v.shape[-1]), S_fin


def hgrn_final_state(k, v, log_f):
    A = jnp.cumsum(log_f, axis=2)
    return jnp.einsum('bhsk,bhsv->bhkv', k * jnp.exp(A[:, :, -1:, :] - A), v)


def hgrn_direction(q, q_c, v, v_c, raw_lat, raw_ctx, lb, reverse):
    log_f, k = hgrn_decay(raw_lat, lb)
    log_fc, k_c = hgrn_decay(raw_ctx, lb)
    order = (lambda t: jnp.flip(t, axis=2)) if reverse else (lambda t: t)
    if q_c is not None:
        s0 = jnp.zeros(k_c.shape[:2] + (HG_HEAD_DIM, HG_HEAD_DIM), jnp.float32)
        o_c, s_ctx = hgrn_chunk_scan(order(q_c), order(k_c), order(v_c), order(log_fc), s0)
        o_c = order(o_c)
    else:
        o_c = None
        s_ctx = hgrn_final_state(order(k_c), order(v_c), order(log_fc))
    o, _ = hgrn_chunk_scan(order(q), order(k), order(v), order(log_f), s_ctx)
    return order(o), o_c


def hgrn_readout(o, g, head_norm_w, w_out):
    o = rms_norm(jnp.swapaxes(o, 1, 2), head_norm_w)
    o = o.reshape(o.shape[:2] + (BRANCH,)).astype(g.dtype) * jax.nn.silu(g)
    return o @ w_out


def hgrn_layer(x, xc, mod, mod_c, norm_w, w_in, lb, head_norm_w, w_out, need_ctx):
    shift, scale, gate = mod
    shift_c, scale_c, gate_c = mod_c
    h = modulate(x, norm_w, shift, scale)
    q, i_lat, f_fwd, f_bwd, g = jnp.split(h @ w_in, 5, axis=-1)
    hc = modulate(xc, norm_w, shift_c, scale_c)
    if need_ctx:
        qc, i_ctx, fc_fwd, fc_bwd, gc = jnp.split(hc @ w_in, 5, axis=-1)
        q_c = hgrn_query(qc)
    else:
        i_ctx, fc_fwd, fc_bwd = jnp.split(hc @ w_in[:, BRANCH:4 * BRANCH], 3, axis=-1)
        q_c = None
    q_h = hgrn_query(q)
    v_h = hg_heads(i_lat.astype(jnp.float32))
    v_c = hg_heads(i_ctx.astype(jnp.float32))
    o_f, oc_f = hgrn_direction(q_h, q_c, v_h, v_c, f_fwd, fc_fwd, lb, False)
    o_b, oc_b = hgrn_direction(q_h, q_c, v_h, v_c, f_bwd, fc_bwd, lb, True)
    x = x + gate * hgrn_readout(o_f + o_b, g, head_norm_w, w_out)
    if need_ctx:
        xc = xc + gate_c * hgrn_readout(oc_f + oc_b, gc, head_norm_w, w_out)
    return x, xc


def setup_inputs(seed: int = 0) -> dict:
    key = jax.random.key(seed)
    ks = jax.random.split(key, 15)
    D, E = D_MODEL, BRANCH
    nrm = jax.random.normal
    return {
        "x": nrm(ks[0], (BATCH, SEQ, D), jnp.float32),
        "c": nrm(ks[1], (BATCH, D), jnp.float32),
        "ctx": nrm(ks[2], (BATCH, CTX_LEN, D), jnp.float32),
        "c_ctx": nrm(ks[3], (D,), jnp.float32),
        "ada_w": nrm(ks[4], (DEPTH, D, 3 * D), jnp.float32) * (0.5 * D ** -0.5),
        "ada_b": nrm(ks[5], (DEPTH, 3 * D), jnp.float32) * 0.02,
        "norm_w": 1.0 + 0.02 * nrm(ks[6], (DEPTH, D), jnp.float32),
        "na_w_in": nrm(ks[7], (N_NA_LAYERS, D, 4 * E), jnp.float32) * D ** -0.5,
        "na_rpb": nrm(ks[8], (N_NA_LAYERS, NA_HEADS, 2 * NA_WIN_H - 1, 2 * NA_WIN_W - 1), jnp.float32) * 0.5,
        "na_w_out": nrm(ks[9], (N_NA_LAYERS, E, D), jnp.float32) * E ** -0.5,
        "hg_w_in": nrm(ks[10], (N_HG_LAYERS, D, 5 * E), jnp.float32) * D ** -0.5,
        "hg_lower": nrm(ks[11], (DEPTH, E), jnp.float32) * 0.5,
        "hg_norm_w": 1.0 + 0.02 * nrm(ks[12], (N_HG_LAYERS, HG_HEAD_DIM), jnp.float32),
        "hg_w_out": nrm(ks[13], (N_HG_LAYERS, E, D), jnp.float32) * E ** -0.5,
        "final_norm_w": 1.0 + 0.02 * nrm(ks[14], (D,), jnp.float32),
    }


def reference(x, c, ctx, c_ctx, ada_w, ada_b, norm_w, na_w_in, na_rpb, na_w_out,
              hg_w_in, hg_lower, hg_norm_w, hg_w_out, final_norm_w):
    lb_all = jnp.cumsum(jax.nn.softmax(hg_lower.astype(jnp.float32), axis=0), axis=0)
    lb_all = lb_all - lb_all[0:1]
    xc = ctx
    for i in range(DEPTH):
        shift, scale, gate = adaln_params(c, ada_w[i], ada_b[i])
        mod = (shift[:, None, :], scale[:, None, :], gate[:, None, :])
        shift_c, scale_c, gate_c = adaln_params(c_ctx, ada_w[i], ada_b[i])
        mod_c = (shift_c, scale_c, gate_c)
        need_ctx = i < DEPTH - 1
        j = i // N_MIXERS
        if i % N_MIXERS == 0:
            x, xc = na_layer(x, xc, mod, mod_c, norm_w[i], na_w_in[j], na_rpb[j], na_w_out[j], need_ctx)
        else:
            x, xc = hgrn_layer(x, xc, mod, mod_c, norm_w[i], hg_w_in[j], lb_all[i], hg_norm_w[j], hg_w_out[j], need_ctx)
    return rms_norm(x, final_norm_w)
```

```python
import contextlib
import numpy as np
import concourse.bass as bass
import concourse.mybir as mybir
from concourse.bass_utils import run_bass_kernel_spmd

F32 = mybir.dt.float32
BF = mybir.dt.bfloat16
AF = mybir.ActivationFunctionType
ALU = mybir.AluOpType

ENGS = ("pe", "act", "dve", "pool", "sp")
COMPUTE = ("pe", "act", "dve", "pool")

NT = 34
TALL = NT * 128
EPS = 1e-6


class Buf:
    __slots__ = ("name", "w", "r")

    def __init__(self, name=""):
        self.name = name
        self.w = None
        self.r = []


class Op:
    __slots__ = ("issuer", "tl", "tidx", "fn", "waits", "signal", "semval", "front", "is_dma")


class Prog:
    NDMA = 24
    EPOCH = 30000

    def __init__(self):
        self.ops = {e: [] for e in ENGS}
        self.tl_last = {}
        self.tl_count = {}
        self.known = {e: {} for e in ENGS}
        self.dma_rr = {"sp": 0, "pool": 0, "act": 0}
        self.barrier_ops = []
        self.barrier_pending = {}
        self.n_ops = 0

    @staticmethod
    def _merge(dst, src):
        for k, v in src.items():
            if dst.get(k, 0) < v:
                dst[k] = v

    def op(self, eng, fn, reads=(), writes=(), dma=False):
        o = Op()
        o.issuer = eng
        o.fn = fn
        o.signal = False
        o.semval = None
        o.is_dma = dma
        o.waits = []
        deps = []
        if dma:
            q = self.dma_rr[eng]
            self.dma_rr[eng] = q + 1
            o.tl = ("dma", eng, q % self.NDMA)
            prev = self.tl_last.get(o.tl)
            if prev is not None:
                deps.append(prev)
        else:
            o.tl = eng
        for b in reads:
            if b.w is not None:
                deps.append(b.w)
        for b in writes:
            if b.w is not None:
                deps.append(b.w)
            deps.extend(b.r)
        known = self.known[eng]
        if self.barrier_pending.get(eng):
            self.barrier_pending[eng] = False
            deps.extend(self.barrier_ops)
        deps.sort(key=lambda d: -d.tidx)
        for d in deps:
            if (not dma) and d.tl == eng:
                if eng == "pe":
                    continue
                pass
            if known.get(d.tl, 0) >= d.tidx:
                continue
            o.waits.append(d)
            d.signal = True
            self._merge(known, d.front)
        self.tl_count[o.tl] = self.tl_count.get(o.tl, 0) + 1
        o.tidx = self.tl_count[o.tl]
        front = {k: v for k, v in known.items() if k in COMPUTE}
        prev = self.tl_last.get(o.tl)
        if prev is not None:
            self._merge(front, prev.front)
        front[o.tl] = o.tidx
        o.front = front
        self.tl_last[o.tl] = o
        if dma:
            o.signal = True
        for b in reads:
            b.r.append(o)
        for b in writes:
            b.w = o
            b.r = []
        self.ops[eng].append(o)
        self.n_ops += 1
        return o

    def barrier(self):
        self.barrier_ops = list(self.tl_last.values())
        self.barrier_pending = {e: True for e in ENGS}

    def emit(self, nc, final_bufs):
        self.op("sp", None, reads=final_bufs)
        sems = {}
        cnt = {}
        stack = contextlib.ExitStack()
        with stack:
            def get_sem(key):
                if key not in sems:
                    nm = "s_" + "_".join(str(k) for k in (key if isinstance(key, tuple) else (key,)))
                    sems[key] = stack.enter_context(nc.semaphore(nm))
                return sems[key]
            for e in ENGS:
                for o in self.ops[e]:
                    if not o.signal:
                        continue
                    c = cnt.get(o.tl, 0) + 1
                    cnt[o.tl] = c
                    if o.is_dma:
                        o.semval = (get_sem(o.tl), 16 * c)
                    else:
                        ep = (c - 1) // self.EPOCH
                        o.semval = (get_sem((o.tl, ep)), (c - 1) % self.EPOCH + 1)
            block = stack.enter_context(nc.Block())
            engmap = {"pe": block.tensor, "act": block.scalar, "dve": block.vector,
                      "pool": block.gpsimd, "sp": block.sync}
            for e in ENGS:
                ops = self.ops[e]

                def body(eng, ops=ops):
                    for o in ops:
                        for d in o.waits:
                            s, v = d.semval
                            eng.wait_ge(s, v)
                        if o.fn is None:
                            continue
                        ins = o.fn(eng)
                        if o.signal:
                            s, v = o.semval
                            ins.then_inc(s, 16 if o.is_dma else 1)
                if ops:
                    engmap[e](body)
        return len(sems)


class Arena:
    def __init__(self, t, nwords):
        self.t = t
        self.n = nwords
        self.off = 0

    def alloc(self, free_shape, dt):
        nel = 1
        for s in free_shape:
            nel *= s
        esz = 4 if dt == F32 else 2
        words = (nel * esz + 3) // 4
        words = (words + 15) // 16 * 16
        assert self.off + words <= self.n, ("arena overflow", self.off, words, self.n)
        a = self.t[:, self.off:self.off + words]
        self.off += words
        if dt != F32:
            a = a.bitcast(dt)
        a = a[:, 0:nel]
        if len(free_shape) == 2:
            a = a.rearrange("p (a b) -> p a b", a=free_shape[0])
        elif len(free_shape) == 3:
            a = a.rearrange("p (a b c) -> p a b c", a=free_shape[0], b=free_shape[1])
        elif len(free_shape) == 4:
            a = a.rearrange("p (a b c d) -> p a b c d", a=free_shape[0], b=free_shape[1], c=free_shape[2])
        return a

    def mark(self):
        return self.off

    def release(self, m):
        self.off = m


class Ring:
    def __init__(self, items):
        self.items = items
        self.i = 0

    def next(self):
        it = self.items[self.i % len(self.items)]
        self.i += 1
        return it


def build(dbg=False, stop_after=None):
    nc = bass.Bass("TRN2", target_bir_lowering=False)
    P = Prog()

    def din(name, shape, dt=F32):
        return nc.dram_tensor(name, list(shape), dt, kind="ExternalInput").ap()

    x_d = din("x", [2, 4096, 1024])
    ctx_d = din("ctx", [2, 256, 1024])
    c_d = din("c", [2, 1024])
    cctx_d = din("c_ctx", [1, 1024])
    adaw_d = din("ada_w", [2, 1024, 3072])
    adab_d = din("ada_b", [2, 3072])
    normw_d = din("norm_w", [2, 1024])
    nawin_d = din("na_w_in", [1024, 4096])
    rpb_d = din("na_rpb", [240, 31])
    nawout_d = din("na_w_out", [1024, 1024])
    hgwin_d = din("hg_w_in", [1024, 5120])
    hglow_d = din("hg_lower", [2, 1024])
    hgnw_d = din("hg_norm_w", [128, 1])
    hgwout_d = din("hg_w_out", [1024, 1024])
    fnw_d = din("final_norm_w", [1, 1024])
    ident_d = din("k_ident", [128, 128])
    ss_d = din("k_ss", [62, 8192])
    nmask_d = din("k_nmask", [128, 64])
    cmask_d = din("k_cmask", [128, 256])
    smask_d = din("k_smask", [128, 512])
    out_d = nc.dram_tensor("out", [2, 4096, 1024], F32, kind="ExternalOutput").ap()

    def dscr(name, shape, dt, out=False):
        if out:
            return nc.dram_tensor(name, list(shape), dt, kind="ExternalOutput").ap()
        return nc.dram_tensor(name, list(shape), dt).ap()

    Xs = dscr("Xs", [2, TALL, 1024], F32, out=dbg)
    Gate = dscr("Gate", [2, 3, 1024], F32)
    QTs = dscr("QTs", [2, NT, 128, 1024], BF)
    KTs = dscr("KTs", [2, NT, 128, 1024], BF)
    VAs = dscr("VAs", [2, TALL, 1024], BF)
    ZTs = dscr("ZTs", [2, NT, 128, 1024], BF)
    QKs = dscr("QKs", [2, NT, 128, 4096], BF)
    SGs = dscr("SGs", [2, 8, 128, 4096], BF)
    Vs = dscr("Vs", [2, TALL, 1024], BF)
    Os = dscr("Os", [2, 2, 8, 128, 4096], F32)
    Es = dscr("Es", [128, 16 * 14 * 64], BF)
    WB = {"nain": dscr("Wb_nain", [1024, 4096], BF), "naout": dscr("Wb_naout", [1024, 1024], BF),
          "hgin": dscr("Wb_hgin", [1024, 5120], BF), "hgout": dscr("Wb_hgout", [1024, 1024], BF)}
    DB = {}
    def dbuf(*key):
        if key not in DB:
            DB[key] = Buf(str(key))
        return DB[key]

    st = contextlib.ExitStack()
    with st:
        AW = 51200
        arena_t = st.enter_context(nc.sbuf_tensor("arena", [128, AW], F32))
        AR = Arena(arena_t, AW)
        pbanks = [st.enter_context(nc.psum_tensor("pb%d" % i, [128, 512], F32))[:] for i in range(8)]
        pbB = [Buf("pb%d" % i) for i in range(8)]

        def dma(eng, out, in_, reads, writes):
            return P.op(eng, lambda e, out=out, in_=in_: e.dma_start(out=out, in_=in_), reads=reads, writes=writes, dma=True)

        def mm(out, lhsT, rhs, start, stop, reads, writes):
            return P.op("pe", lambda e, out=out, lhsT=lhsT, rhs=rhs, start=start, stop=stop:
                        e.matmul(out, lhsT=lhsT, rhs=rhs, start=start, stop=stop), reads=reads, writes=writes)

        def tr(out, in_, ident, reads, writes):
            return P.op("pe", lambda e, out=out, in_=in_, ident=ident: e.transpose(out=out, in_=in_, identity=ident),
                        reads=reads, writes=writes)

        def act(out, in_, func, reads, writes, scale=1.0, bias=0.0, accum=None):
            def f(e, out=out, in_=in_, func=func, scale=scale, bias=bias, accum=accum):
                kw = {}
                if accum is not None:
                    kw["accum_out"] = accum
                return e.activation(out=out, in_=in_, func=func, bias=bias, scale=scale, **kw)
            return P.op("act", f, reads=reads, writes=writes)

        def tt(eng, out, in0, in1, op, reads, writes):
            return P.op(eng, lambda e, out=out, in0=in0, in1=in1, op=op: e.tensor_tensor(out=out, in0=in0, in1=in1, op=op),
                        reads=reads, writes=writes)

        def ts(eng, out, in0, s1, s2, op0, op1, reads, writes):
            if s2 is None:
                return P.op(eng, lambda e, out=out, in0=in0, s1=s1, op0=op0:
                            e.tensor_scalar(out=out, in0=in0, scalar1=s1, scalar2=None, op0=op0), reads=reads, writes=writes)
            return P.op(eng, lambda e, out=out, in0=in0, s1=s1, s2=s2, op0=op0, op1=op1:
                        e.tensor_scalar(out=out, in0=in0, scalar1=s1, scalar2=s2, op0=op0, op1=op1), reads=reads, writes=writes)

        def stt(out, in0, scalar, in1, op0, op1, reads, writes):
            return P.op("dve", lambda e, out=out, in0=in0, scalar=scalar, in1=in1, op0=op0, op1=op1:
                        e.scalar_tensor_tensor(out=out, in0=in0, scalar=scalar, in1=in1, op0=op0, op1=op1),
                        reads=reads, writes=writes)

        def cp(eng, out, in_, reads, writes):
            if eng == "act":
                return P.op("act", lambda e, out=out, in_=in_: e.activation(out=out, in_=in_, func=AF.Copy), reads=reads, writes=writes)
            return P.op(eng, lambda e, out=out, in_=in_: e.tensor_copy(out=out, in_=in_), reads=reads, writes=writes)

        def memset(eng, ap, val, writes):
            return P.op(eng, lambda e, ap=ap, val=val: e.memset(ap, val), writes=writes)

        ident = AR.alloc([128], F32); identB = Buf("ident")
        identb = AR.alloc([128], BF); identbB = Buf("identb")
        ones = AR.alloc([128], F32); onesB = Buf("ones")
        onesb = AR.alloc([128], BF)
        R0T = AR.alloc([8, 8], F32); R0TB = Buf("R0T")
        scT = AR.alloc([8, 3], F32); scTB = Buf("scT")
        lbT = AR.alloc([8], F32); omlT = AR.alloc([8], F32); nomlT = AR.alloc([8], F32); lbB = Buf("lb")
        MOD = [AR.alloc([24, 3], F32) for _ in range(2)]; MODB = [Buf("mod0"), Buf("mod1")]
        A1 = [AR.alloc([8, 3], F32) for _ in range(2)]; A1B = [Buf("a10"), Buf("a11")]
        hgnw = AR.alloc([1], F32); hgnwB = Buf("hgnw")
        cmask = AR.alloc([2, 128], F32); cmaskB = Buf("cmask")
        smask = AR.alloc([512], F32); smaskB = Buf("smask")
        epsb = AR.alloc([1], F32); epsB = Buf("eps")
        dma("sp", ident, ident_d, [], [identB])
        dma("sp", hgnw, hgnw_d, [], [hgnwB])
        dma("sp", cmask.rearrange("p a b -> p (a b)"), cmask_d, [], [cmaskB])
        dma("sp", smask, smask_d, [], [smaskB])
        cp("dve", identb, ident, [identB], [identbB])
        memset("pool", ones, 1.0, [onesB])
        memset("pool", onesb, 1.0, [onesB])
        memset("pool", epsb, EPS, [epsB])

        persist_mark = AR.mark()

        stg_ring = Ring([(AR.alloc([1024], F32), Buf("stg%d" % i)) for i in range(6)])
        stb_ring = Ring([(AR.alloc([1024], BF), Buf("stb%d" % i)) for i in range(6)])
        ci_ = 0
        for name, src_d, ncols in (("nain", nawin_d, 4096), ("naout", nawout_d, 1024), ("hgin", hgwin_d, 5120), ("hgout", hgwout_d, 1024)):
            for c0 in range(0, ncols, 1024):
                for k in range(8):
                    stg, stgB = stg_ring.next()
                    stb, stbB = stb_ring.next()
                    dma("sp", stg, src_d[k * 128:(k + 1) * 128, c0:c0 + 1024], [], [stgB])
                    cp(("act", "dve", "pool")[ci_ % 3], stb, stg, [stgB], [stbB])
                    ci_ += 1
                    dma("act", WB[name][k * 128:(k + 1) * 128, c0:c0 + 1024], stb, [stbB], [dbuf("Wb", name, k, c0 // 1024)])

        R0 = AR.alloc([1024], F32); R0B = Buf("R0")
        memset("pool", R0[0:8, :], 0.0, [R0B])
        dma("sp", R0[0:2, :], c_d, [], [R0B])
        dma("sp", R0[2:3, :], cctx_d, [], [R0B])
        dma("sp", R0[3:5, :], normw_d, [], [R0B])
        dma("sp", R0[5:7, :], hglow_d, [], [R0B])
        tp = pbanks[0]
        for k in range(8):
            tr(tp[:, k * 8:(k + 1) * 8], R0[0:8, k * 128:(k + 1) * 128], ident[0:8, 0:8], [R0B, identB], [pbB[0]])
        cp("dve", R0T.rearrange("p a b -> p (a b)"), tp[:, 0:64], [pbB[0]], [R0TB])
        act(scT, R0T[:, :, 0:3], AF.Silu, [R0TB], [scTB])
        tt("dve", lbT, R0T[:, :, 6], R0T[:, :, 5], ALU.subtract, [R0TB], [lbB])
        act(lbT, lbT, AF.Sigmoid, [lbB], [lbB])
        ts("dve", omlT, lbT, -1.0, 1.0, ALU.mult, ALU.add, [lbB], [lbB])
        ts("dve", nomlT, omlT, -1.0, None, ALU.mult, None, [lbB], [lbB])

        Mrow = AR.alloc([3072], F32); MrowB = Buf("Mrow")
        adab = AR.alloc([3072], F32); adabB = Buf("adab")
        awr = Ring([(AR.alloc([8, 512], F32), Buf("aw%d" % i)) for i in range(2)])
        for i in range(2):
            dma("sp", adab[0:3, :], adab_d[i:i + 1, :].partition_broadcast(3), [], [adabB])
            for cb in range(6):
                aw, awB = awr.next()
                dma("sp", aw, adaw_d[i, :, cb * 512:(cb + 1) * 512].rearrange("(k p) n -> p k n", p=128), [], [awB])
                mp = pbanks[1 + (cb % 2)]; mpB = pbB[1 + (cb % 2)]
                for k in range(8):
                    mm(mp[0:3, :], scT[:, k, :], aw[:, k, :], k == 0, k == 7, [scTB, awB], [mpB])
                tt("dve", Mrow[0:3, cb * 512:(cb + 1) * 512], mp[0:3, :], adab[0:3, cb * 512:(cb + 1) * 512], ALU.add,
                   [mpB, adabB], [MrowB])
            dma("sp", Gate[i], Mrow[0:3, 2048:3072], [MrowB], [dbuf("gate", i)])
            tp = pbanks[3]
            for ch in range(24):
                tr(tp[:, ch * 3:(ch + 1) * 3], Mrow[0:3, ch * 128:(ch + 1) * 128], ident[0:3, 0:3], [MrowB, identB], [pbB[3]])
            cp("dve", MOD[i].rearrange("p a b -> p (a b)"), tp[:, 0:72], [pbB[3]], [MODB[i]])
            ts("dve", A1[i], MOD[i][:, 8:16, :], 1.0, None, ALU.add, None, [MODB[i]], [A1B[i]])
            nb = bass.AP(R0T.tensor, R0T[:, 0, 3 + i].offset, [[R0T.ap[0][0], 128], [8, 8], [0, 3]])
            tt("dve", A1[i], A1[i], nb, ALU.mult, [A1B[i], R0TB], [A1B[i]])
        AR.release(persist_mark)
        P.barrier()

        def token_src(layer, b, t):
            if layer == 0:
                if t < 32:
                    return x_d[b, t * 128:(t + 1) * 128, :], None
                return ctx_d[b, (t - 32) * 128:(t - 31) * 128, :], None
            return Xs[b, t * 128:(t + 1) * 128, :], dbuf("Xs", b, t)

        def make_hT(layer, b, tiles, xin_ring, hT_ring, junk, stat_ring):
            j = b if tiles[0] < 32 else 2
            ntok = 128 * len(tiles)
            xs = []
            for t in tiles:
                xin, xinB = xin_ring.next()
                src, sB = token_src(layer, b, t)
                dma("sp", xin, src, [sB] if sB else [], [xinB])
                stt_, stB = stat_ring.next()
                act(junk[0], xin, AF.Square, [xinB], [junk[1], stB], accum=stt_[:, 0:1])
                ts("dve", stt_[:, 1:2], stt_[:, 0:1], 1.0 / 1024, EPS, ALU.mult, ALU.add, [stB], [stB])
                act(stt_[:, 2:3], stt_[:, 1:2], AF.Ln, [stB], [stB])
                act(stt_[:, 3:4], stt_[:, 2:3], AF.Exp, [stB], [stB], scale=-0.5)
                ts("pool", xin, xin, stt_[:, 3:4], 1.0, ALU.mult, ALU.mult, [xinB, stB], [xinB])
                xs.append((xin, xinB))
            hT, hTB = hT_ring.next()
            for k in range(8):
                tpb = k % 2
                tp = pbanks[tpb]
                for ti, (xin, xinB) in enumerate(xs):
                    tr(tp[:, ti * 128:(ti + 1) * 128], xin[:, k * 128:(k + 1) * 128], ident, [xinB, identB], [pbB[tpb]])
                if k % 2 == 0:
                    act(hT[:, k, 0:ntok], tp[:, 0:ntok], AF.Identity, [pbB[tpb], A1B[layer], MODB[layer]], [hTB],
                        scale=A1[layer][:, k, j:j + 1], bias=MOD[layer][:, k, j:j + 1])
                else:
                    ts("dve", hT[:, k, 0:ntok], tp[:, 0:ntok], A1[layer][:, k, j:j + 1], MOD[layer][:, k, j:j + 1],
                       ALU.mult, ALU.add, [pbB[tpb], A1B[layer], MODB[layer]], [hTB])
            return hT, hTB

        class WBufs:
            def __init__(self, n):
                self.b = [Buf("w%d" % i) for i in range(n)]

        def load_w_bf16(dst, name, ncols):
            wb = WBufs(ncols // 512)
            src = WB[name]
            for cb in range(ncols // 512):
                dma("sp", dst[:, :, cb * 512:(cb + 1) * 512], src[:, cb * 512:(cb + 1) * 512].rearrange("(k p) n -> p k n", p=128),
                    [dbuf("Wb", name, k, cb // 2) for k in range(8)], [wb.b[cb]])
            return wb

        BLOCKS = [list(range(i * 4, i * 4 + 4)) for i in range(8)] + [[32, 33]]

        def na_layer(b):
            m0 = AR.mark()
            Win = AR.alloc([8, 4096], BF)
            WinW = load_w_bf16(Win, "nain", 4096)
            xin_ring = Ring([(AR.alloc([1024], F32), Buf("xin%d" % i)) for i in range(6)])
            hT_ring = Ring([(AR.alloc([8, 512], BF), Buf("hT%d" % i)) for i in range(2)])
            junk = (AR.alloc([1024], BF), Buf("junk"))
            stat_ring = Ring([(AR.alloc([4], F32), Buf("st%d" % i)) for i in range(8)])
            qst_ring = Ring([(AR.alloc([8, 512], BF), Buf("qst%d" % i)) for i in range(2)])
            kst_ring = Ring([(AR.alloc([8, 512], BF), Buf("kst%d" % i)) for i in range(2)])
            vst_ring = Ring([(AR.alloc([4, 1024], BF), Buf("vst%d" % i)) for i in range(2)])
            zst_ring = Ring([(AR.alloc([8, 512], BF), Buf("zst%d" % i)) for i in range(2)])
            pring = Ring([(pbanks[i], pbB[i]) for i in range(2, 8)])
            ev = 0
            for tiles in BLOCKS:
                ntok = 128 * len(tiles)
                hT, hTB = make_hT(0, b, tiles, xin_ring, hT_ring, junk, stat_ring)
                qst, qstB = qst_ring.next()
                kst, kstB = kst_ring.next()
                vst, vstB = vst_ring.next()
                zst, zstB = zst_ring.next()
                for jc in list(range(16)) + list(range(24, 32)):
                    pp, ppB = pring.next()
                    for k in range(8):
                        mm(pp[:, 0:ntok], Win[:, k, jc * 128:(jc + 1) * 128], hT[:, k, 0:ntok], k == 0, k == 7, [WinW.b[jc // 4], hTB], [ppB])
                    if jc >= 24:
                        act(zst[:, jc - 24, 0:ntok], pp[:, 0:ntok], AF.Silu, [ppB], [zstB])
                        continue
                    dst, dB = (qst, qstB) if jc < 8 else (kst, kstB)
                    eng = "act" if ev % 2 == 0 else "dve"
                    ev += 1
                    cp(eng, dst[:, jc % 8, 0:ntok], pp[:, 0:ntok], [ppB], [dB])
                for ti, t in enumerate(tiles):
                    dma("act", QTs[b, t].rearrange("p (c n) -> p c n", c=8), qst[:, :, ti * 128:(ti + 1) * 128], [qstB], [dbuf("QT", b, t)])
                    dma("act", KTs[b, t].rearrange("p (c n) -> p c n", c=8), kst[:, :, ti * 128:(ti + 1) * 128], [kstB], [dbuf("KT", b, t)])
                    dma("act", ZTs[b, t].rearrange("p (c n) -> p c n", c=8), zst[:, :, ti * 128:(ti + 1) * 128], [zstB], [dbuf("ZT", b, t)])
                for ti, t in enumerate(tiles):
                    for half in range(2):
                        pp, ppB = pring.next()
                        c0 = 2048 + half * 512
                        for k in range(8):
                            mm(pp, hT[:, k, ti * 128:(ti + 1) * 128], Win[:, k, c0:c0 + 512], k == 0, k == 7, [WinW.b[c0 // 512], hTB], [ppB])
                        eng = "dve" if ev % 2 == 0 else "act"
                        ev += 1
                        cp(eng, vst[:, ti, half * 512:(half + 1) * 512], pp, [ppB], [vstB])
                    dma("act", VAs[b, t * 128:(t + 1) * 128, :], vst[:, ti, :], [vstB], [dbuf("VA", b, t)])
            AR.release(m0)
            P.barrier()
            if stop_after == "na1":
                return

            m0 = AR.mark()
            E = AR.alloc([16 * 14, 64], BF); EB = Buf("E")
            if b == 0:
                m1 = AR.mark()
                X2 = AR.alloc([2, 62], F32); X2B = Buf("X2")
                R2 = AR.alloc([240], F32); R2B = Buf("R2")
                SS = AR.alloc([64, 128], F32); SSB = Buf("SS")
                nmask = AR.alloc([64], F32); nmaskB = Buf("nmask")
                Eraw = AR.alloc([240, 64], F32); ErawB = Buf("Eraw")
                memset("pool", X2.rearrange("p a b -> p (a b)"), 0.0, [X2B])
                for tI in range(2):
                    dma("sp", X2[0:120, tI, 0:31], rpb_d[tI * 120:(tI + 1) * 120, :], [], [X2B])
                    n2 = 120 if tI == 0 else 119
                    dma("sp", X2[0:n2, tI, 31:62], rpb_d[tI * 120 + 1:tI * 120 + 1 + n2, :], [], [X2B])
                dma("sp", SS[0:62, :, :].rearrange("p a b -> p (a b)"), ss_d, [], [SSB])
                dma("sp", nmask, nmask_d, [], [nmaskB])
                for tI in range(2):
                    tr(pbanks[0][0:62, tI * 120:(tI + 1) * 120], X2[0:120, tI, :], ident[0:120, 0:120], [X2B, identB], [pbB[0]])
                cp("dve", R2[0:62, :], pbanks[0][0:62, 0:240], [pbB[0]], [R2B])
                for qc in range(64):
                    bk = qc % 2
                    mm(pbanks[bk][:, 0:240], SS[0:62, qc, :], R2[0:62, :], True, True, [SSB, R2B], [pbB[bk]])
                    act(Eraw[:, :, qc], pbanks[bk][:, 0:240], AF.Exp, [pbB[bk]], [ErawB])
                for h in range(16):
                    nmb = bass.AP(nmask.tensor, nmask.offset, [[nmask.ap[0][0], 128], [0, 14], [1, 64]])
                    er = bass.AP(Eraw.tensor, Eraw[:, h * 15 + 13, :].offset, [[Eraw.ap[0][0], 128], [-64, 14], [1, 64]])
                    tt("dve" if h % 2 else "pool", E[:, h * 14:(h + 1) * 14, :], er, nmb, ALU.mult,
                       [ErawB, nmaskB], [EB])
                AR.release(m1)

                dma("sp", Es, E.rearrange("p a b -> p (a b)"), [EB], [dbuf("Es")])
            else:
                dma("sp", E.rearrange("p a b -> p (a b)"), Es, [dbuf("Es")], [EB])
            P.barrier()

            Wo = AR.alloc([8, 1024], BF)
            WoW = load_w_bf16(Wo, "naout", 1024)
            gate = {}
            gateB = Buf("gatebc")
            for j in (b, 2):
                gate[j] = AR.alloc([1024], F32)
                dma("sp", gate[j], Gate[0, j:j + 1, :].partition_broadcast(128), [dbuf("gate", 0)], [gateB])
            NKS = 12
            ktr = [(AR.alloc([8, 128], BF), Buf("kt%d" % i)) for i in range(NKS)]
            v2r = [(AR.alloc([16, 128], BF), Buf("v2%d" % i)) for i in range(NKS)]
            ktc = [(AR.alloc([8, 128], BF), Buf("ktc%d" % i)) for i in range(2)]
            v2c = [(AR.alloc([16, 128], BF), Buf("v2c%d" % i)) for i in range(2)]
            for v2, vB in v2r + v2c:
                v4 = v2.rearrange("p (h2 two) e -> p h2 two e", two=2)
                memset("pool", v4[:, :, 0, 64:128], 1.0, [vB])
                memset("pool", v4[:, :, 1, 0:64], 1.0, [vB])
            q_ring = Ring([(AR.alloc([8, 2, 512], BF), Buf("q%d" % i)) for i in range(1)])
            for qg_, qB_ in q_ring.items:
                memset("pool", qg_.rearrange("p a b c -> p (a b c)"), 0.0, [qB_])
            z_ring = Ring([(AR.alloc([8, 512], BF), Buf("z%d" % i)) for i in range(1)])
            gt_ring = Ring([(AR.alloc([8, 512], BF), Buf("gt%d" % i)) for i in range(1)])
            pt_ring = Ring([(AR.alloc([512], BF), Buf("pt%d" % i)) for i in range(9)])
            rc_ring = Ring([(AR.alloc([512], F32), Buf("rc%d" % i)) for i in range(1)])
            tm_ring = Ring([(AR.alloc([512], F32), Buf("tm%d" % i)) for i in range(1)])
            xo_ring = Ring([(AR.alloc([1024], F32), Buf("xo%d" % i)) for i in range(2)])
            yt_ring = Ring([(AR.alloc([512], F32), Buf("yt%d" % i)) for i in range(2)])
            s_ring = Ring([(pbanks[i], pbB[i]) for i in range(5)])
            obank = [(pbanks[5], pbB[5]), (pbanks[6], pbB[6])]
            y_ring = Ring([(pbanks[i], pbB[i]) for i in (7,)])
            LOOK = 8
            loaded = {}

            def key_tile(a):
                if a in loaded:
                    return loaded[a]
                if a >= 32:
                    kt, kB = ktc[a - 32]
                    v2, vB = v2c[a - 32]
                else:
                    kt, kB = ktr[a % NKS]
                    v2, vB = v2r[a % NKS]
                dma("sp", kt, KTs[b, a].rearrange("p (c n) -> p c n", c=8), [dbuf("KT", b, a)], [kB])
                src = VAs[b, a * 128:(a + 1) * 128, :].rearrange("p (h2 two d) -> p h2 two d", two=2, d=64)
                v4 = v2.rearrange("p (h2 two) e -> p h2 two e", two=2)
                dma("sp", v4[:, :, 0, 0:64], src[:, :, 0, :], [dbuf("VA", b, a)], [vB])
                dma("sp", v4[:, :, 1, 64:128], src[:, :, 1, :], [dbuf("VA", b, a)], [vB])
                for old in [x for x in loaded if x < 32 and (x % NKS) == (a % NKS)]:
                    del loaded[old]
                loaded[a] = (kt, kB, v2, vB)
                return loaded[a]

            def r0_of(r):
                return min(max(r - 4, 0), 56)

            def group_tiles(g):
                if g == 8:
                    return []
                res_ = []
                for a in range(32):
                    full, up, lo = [], [], []
                    for r in range(8 * g, 8 * g + 8):
                        r0 = r0_of(r)
                        n0 = r0 <= 2 * a <= r0 + 7
                        n1 = r0 <= 2 * a + 1 <= r0 + 7
                        if n0 and n1:
                            full.append(r)
                        elif n1:
                            up.append(r)
                        elif n0:
                            lo.append(r)
                    allr = sorted(full + up + lo)
                    if not allr:
                        continue
                    assert allr == list(range(allr[0], allr[-1] + 1))
                    segs = []
                    for rows, (plo, phi) in ((full, (0, 128)), (up, (64, 128)), (lo, (0, 64))):
                        if rows:
                            assert rows == list(range(rows[0], rows[-1] + 1))
                            segs.append((plo, phi, rows[0], rows[-1]))
                    res_.append((a, allr[0], allr[-1], segs))
                return res_

            key_tile(32); key_tile(33)
            qcur = None
            for g in range(9):
                qtiles = list(range(4 * g, 4 * g + 4)) if g < 8 else [32, 33]
                n = 128 * len(qtiles)
                jg = b if g < 8 else 2
                rbase = 8 * g
                for gg in (g, g + 1):
                    if gg < 8:
                        for (a, _, _, _) in group_tiles(gg):
                            key_tile(a)
                qg, qB = q_ring.next()
                zg, zB = z_ring.next()
                for ti, t in enumerate(qtiles):
                    qsrc = QTs[b, t].rearrange("p (c n) -> p c n", c=8)
                    dma("sp", qg[0:64, :, 0, ti * 128:(ti + 1) * 128], qsrc[0:64], [dbuf("QT", b, t)], [qB])
                    dma("sp", qg[64:128, :, 1, ti * 128:(ti + 1) * 128], qsrc[64:128], [dbuf("QT", b, t)], [qB])
                for ti, t in enumerate(qtiles):
                    dma("sp", zg[:, :, ti * 128:(ti + 1) * 128], ZTs[b, t].rearrange("p (c n) -> p c n", c=8), [dbuf("ZT", b, t)], [zB])
                gt, gtB = gt_ring.next()
                gtiles = group_tiles(g)
                items = []
                for (a, u_lo, u_hi, segs) in gtiles:
                    items.append((a, (u_lo - rbase) * 64, (u_hi - rbase + 1) * 64,
                                  [(plo, phi, (r_lo - rbase) * 64, (r_hi - rbase + 1) * 64) for (plo, phi, r_lo, r_hi) in segs],
                                  2 * a - u_lo + 7, u_hi - u_lo + 1))
                for a in (32, 33):
                    items.append((a, 0, n, [(0, 128, 0, n)], None, None))
                nit = len(items)
                pend = []

                def emit_pv(rec):
                    h, idx, (a, c0, c1, segs, dlo, nU), pt, ptB, v2, vB = rec
                    ob, oB = obank[h % 2]
                    P.op("pe", lambda e, out=ob[:, c0:c1], lhsT=v2[:, h, :], rhs=pt[:, 0:c1 - c0], st_=(idx == 0), sp_=(idx == nit - 1):
                         e.matmul(out, lhsT=lhsT, rhs=rhs, start=st_, stop=sp_, skip_group_check=True), reads=[vB, ptB], writes=[oB])
                    if idx == nit - 1:
                        c, hf = h // 2, (h % 2) * 64
                        dh = 64 - hf
                        rc, rcB = rc_ring.next()
                        tm, tmB = tm_ring.next()
                        act(rc[hf:hf + 64, 0:n], ob[dh:dh + 64, 0:n], AF.Ln, [], [rcB, oB])
                        act(rc[hf:hf + 64, 0:n], rc[hf:hf + 64, 0:n], AF.Exp, [rcB], [rcB], scale=-1.0)
                        tt("dve", tm[hf:hf + 64, 0:n], ob[hf:hf + 64, 0:n], rc[hf:hf + 64, 0:n], ALU.mult, [rcB], [tmB, oB])
                        tt("pool", gt[hf:hf + 64, c, 0:n], tm[hf:hf + 64, 0:n], zg[hf:hf + 64, c, 0:n], ALU.mult, [tmB, zB], [gtB])
                for h in range(16):
                    c = h // 2
                    hf_ = (h % 2) * 64
                    for idx, it in enumerate(items):
                        (a, c0, c1, segs, dlo, nU) = it
                        kt, kB, v2, vB = key_tile(a)
                        sb_, sbB = s_ring.next()
                        w = c1 - c0
                        mm(sb_[:, 0:w], kt[:, c, :], qg[:, c, h % 2, c0:c1], True, True, [kB, qB], [sbB])
                        pt, ptB = pt_ring.next()
                        act(pt[:, 0:w], sb_[:, 0:w], AF.Exp, [], [ptB, sbB], scale=0.125)
                        if dlo is not None:
                            assert 0 <= dlo - (nU - 1) and dlo <= 13
                            e0 = h * 14 + 13 - dlo
                            e_ap = E[:, e0:e0 + nU, :].rearrange("p r q -> p (r q)")
                            em_eng = "pool" if (idx % 3 == 1) else "dve"
                            tt(em_eng, pt[:, 0:w], pt[:, 0:w], e_ap, ALU.mult, [ptB, EB], [ptB])
                            for (plo, phi, s0, s1) in segs:
                                if (plo, phi) == (64, 128):
                                    memset(em_eng, pt[0:64, s0 - c0:s1 - c0], 0.0, [ptB])
                                elif (plo, phi) == (0, 64):
                                    memset(em_eng, pt[64:128, s0 - c0:s1 - c0], 0.0, [ptB])
                        pend.append((h, idx, it, pt, ptB, v2, vB))
                        if len(pend) > LOOK:
                            emit_pv(pend.pop(0))
                while pend:
                    emit_pv(pend.pop(0))
                for ti, t in enumerate(qtiles):
                    xo, xoB = xo_ring.next()
                    src, sB = token_src(0, b, t)
                    dma("sp", xo, src, [], [xoB])
                    for half in range(2):
                        yt, ytB = yt_ring.next()
                        yp, ypB = y_ring.next()
                        for k in range(8):
                            mm(yp, gt[:, k, ti * 128:(ti + 1) * 128], Wo[:, k, half * 512:(half + 1) * 512], k == 0, k == 7, [gtB, WoW.b[half]], [ypB])
                        tt("dve", yt, yp, gate[jg][:, half * 512:(half + 1) * 512], ALU.mult, [gateB], [ytB, ypB])
                        tt("pool", xo[:, half * 512:(half + 1) * 512], xo[:, half * 512:(half + 1) * 512], yt, ALU.add, [xoB, ytB], [xoB])
                    dma("pool", Xs[b, t * 128:(t + 1) * 128, :], xo, [xoB], [dbuf("Xs", b, t)])
            AR.release(m0)
            P.barrier()

        def out_proj(layer, b, wout_d, gsrc, tiles, final, ghT=None):
            Wo = AR.alloc([8, 1024], BF)
            WoW = load_w_bf16(Wo, wout_d, 1024)
            gate = [None] * 3
            gateB = Buf("gatebc")
            for j in ([b, 2] if not final else [b]):
                gate[j] = AR.alloc([1024], F32)
                dma("sp", gate[j], Gate[layer, j:j + 1, :].partition_broadcast(128), [dbuf("gate", layer)], [gateB])
            if final:
                fnw = AR.alloc([1024], F32)
                dma("sp", fnw, fnw_d[0:1, :].partition_broadcast(128), [], [gateB])
            gin_ring = Ring([(AR.alloc([1024], BF), Buf("gin%d" % i)) for i in range(2)])
            gT_ring = Ring([(AR.alloc([8, 128], BF), Buf("gT%d" % i)) for i in range(2)])
            xo_ring = Ring([(AR.alloc([1024], F32), Buf("xo%d" % i)) for i in range(3)])
            tmp_ring = Ring([(AR.alloc([1024], F32), Buf("tmp%d" % i)) for i in range(2)])
            st_ring = Ring([(AR.alloc([4], F32), Buf("fst%d" % i)) for i in range(4)])
            junk = (AR.alloc([1024], BF), Buf("junk2"))
            y_ring = Ring([(pbanks[i], pbB[i]) for i in range(2, 6)])
            tp_ring = Ring([(pbanks[i], pbB[i]) for i in range(2)])
            for t in tiles:
                j = b if t < 32 else 2
                if ghT is None:
                    gin, ginB = gin_ring.next()
                    src, sBs = gsrc(t)
                    dma("sp", gin, src, sBs, [ginB])
                    tpp, tppB = tp_ring.next()
                    tpb = tpp[:].bitcast(BF)
                    for k in range(8):
                        tr(tpb[:, k * 128:(k + 1) * 128], gin[:, k * 128:(k + 1) * 128], identb, [ginB, identbB], [tppB])
                    gT, gTB = gT_ring.next()
                    cp("act", gT.rearrange("p a b -> p (a b)"), tpb, [tppB], [gTB])
                    lhs = lambda k: gT[:, k, :]
                else:
                    gfull, gTB, tcol = ghT(t)
                    lhs = lambda k: gfull[:, k, tcol:tcol + 128]
                xo, xoB = xo_ring.next()
                src, sB = token_src(layer, b, t)
                dma("sp", xo, src, [sB] if sB else [], [xoB])
                tmp, tmpB = tmp_ring.next()
                for half in range(2):
                    yp, ypB = y_ring.next()
                    for k in range(8):
                        mm(yp, lhs(k), Wo[:, k, half * 512:(half + 1) * 512], k == 0, k == 7, [gTB, WoW.b[half]], [ypB])
                    tt("dve", tmp[:, half * 512:(half + 1) * 512], yp, gate[j][:, half * 512:(half + 1) * 512], ALU.mult,
                       [ypB, gateB], [tmpB])
                tt("pool", xo, xo, tmp, ALU.add, [xoB, tmpB], [xoB])
                if not final:
                    dma("pool", Xs[b, t * 128:(t + 1) * 128, :], xo, [xoB], [dbuf("Xs", b, t)])
                else:
                    stt_, stB = st_ring.next()
                    act(junk[0], xo, AF.Square, [xoB], [junk[1], stB], accum=stt_[:, 0:1])
                    ts("dve", stt_[:, 1:2], stt_[:, 0:1], 1.0 / 1024, EPS, ALU.mult, ALU.add, [stB], [stB])
                    act(stt_[:, 2:3], stt_[:, 1:2], AF.Ln, [stB], [stB])
                    act(stt_[:, 3:4], stt_[:, 2:3], AF.Exp, [stB], [stB], scale=-0.5)
                    stt(tmp, xo, stt_[:, 3:4], fnw, ALU.mult, ALU.mult, [xoB, stB, gateB], [tmpB])
                    dma("pool", out_d[b, t * 128:(t + 1) * 128, :], tmp, [tmpB], [dbuf("out", b, t)])

        GL = None

        def hg_layer(b):
            m0 = AR.mark()
            GLt = AR.alloc([2, 8, 68], F32); GLB = Buf("GL")
            m_gl = AR.mark()
            Win = AR.alloc([8, 5120], BF)
            WinW = load_w_bf16(Win, "hgin", 5120)
            xin_ring = Ring([(AR.alloc([1024], F32), Buf("xin%d" % i)) for i in range(5)])
            hT_ring = Ring([(AR.alloc([8, 512], BF), Buf("hT%d" % i)) for i in range(2)])
            junk = (AR.alloc([1024], BF), Buf("junk"))
            stat_ring = Ring([(AR.alloc([4], F32), Buf("st%d" % i)) for i in range(8)])
            qk_ring = Ring([(AR.alloc([2, 2, 512], BF), Buf("qkst%d" % i)) for i in range(2)])
            sg_ring = Ring([(AR.alloc([512], BF), Buf("sgst%d" % i)) for i in range(2)])
            vst_ring = Ring([(AR.alloc([4, 1024], BF), Buf("vst%d" % i)) for i in range(2)])
            qs_ring = Ring([(AR.alloc([512], F32), Buf("qs%d" % i)) for i in range(3)])

            def tmpring(nm, cnt_):
                return Ring([(AR.alloc([512], F32), Buf("%s%d" % (nm, i))) for i in range(cnt_)])
            sgq_ring = tmpring("sgq", 2)
            sig_ring, lf_ring, kk_ring, A_ring = [tmpring(n, 4) for n in ("sig", "lf", "kk", "A")]
            pring = Ring([(pbanks[i], pbB[i]) for i in range(2, 8)])
            ev = 0
            for tiles in BLOCKS:
                nt_ = len(tiles)
                ntok = 128 * nt_
                hT, hTB = make_hT(1, b, tiles, xin_ring, hT_ring, junk, stat_ring)

                def proj_fm(col0):
                    pp, ppB = pring.next()
                    for k in range(8):
                        mm(pp[:, 0:ntok], Win[:, k, col0:col0 + 128], hT[:, k, 0:ntok], k == 0, k == 7, [WinW.b[col0 // 512], hTB], [ppB])
                    return pp, ppB
                vst, vstB = vst_ring.next()
                for ti, t in enumerate(tiles):
                    for half in range(2):
                        pp, ppB = pring.next()
                        c0 = 1024 + half * 512
                        for k in range(8):
                            mm(pp, hT[:, k, ti * 128:(ti + 1) * 128], Win[:, k, c0:c0 + 512], k == 0, k == 7, [WinW.b[c0 // 512], hTB], [ppB])
                        eng = "dve" if ev % 2 == 0 else "act"
                        ev += 1
                        cp(eng, vst[:, ti, half * 512:(half + 1) * 512], pp, [ppB], [vstB])
                    dma("sp", Vs[b, t * 128:(t + 1) * 128, :], vst[:, ti, :], [vstB], [dbuf("V", b, t)])
                for hp in range(4):
                    if stop_after == "hg1_v":
                        break
                    pair = (2 * hp, 2 * hp + 1)
                    st_ = {}
                    for h in pair:
                        qkst, qkB = qk_ring.next()
                        pp, ppB = proj_fm(h * 128)
                        sq_, sqB_ = sgq_ring.next()
                        act(sq_[:, 0:ntok], pp[:, 0:ntok], AF.Sigmoid, [], [sqB_, ppB])
                        qs, qsB = qs_ring.next()
                        tt("dve", qs[:, 0:ntok], pp[:, 0:ntok], sq_[:, 0:ntok], ALU.mult, [sqB_], [qsB, ppB])
                        pp, ppB = proj_fm(4096 + h * 128)
                        sq_, sqB_ = sgq_ring.next()
                        act(sq_[:, 0:ntok], pp[:, 0:ntok], AF.Sigmoid, [], [sqB_, ppB])
                        sg, sgB = sg_ring.next()
                        tt("dve", sg[:, 0:ntok], pp[:, 0:ntok], sq_[:, 0:ntok], ALU.mult, [sqB_], [sgB, ppB])
                        if tiles[0] < 32:
                            dma("sp", SGs[b, h, :, tiles[0] * 128:tiles[0] * 128 + ntok], sg[:, 0:ntok], [sgB], [dbuf("SG", b, h, tiles[0] // 4)])
                        sigs = []
                        for d in range(2):
                            pp, ppB = proj_fm(2048 + d * 1024 + h * 128)
                            sig, sigB = sig_ring.next()
                            act(sig[:, 0:ntok], pp[:, 0:ntok], AF.Sigmoid, [], [sigB, ppB])
                            kk, kkB = kk_ring.next()
                            ts("pool", kk[:, 0:ntok], sig[:, 0:ntok], nomlT[:, h:h + 1], omlT[:, h:h + 1], ALU.mult, ALU.add, [sigB, lbB], [kkB])
                            sigs.append((sig, sigB, kk, kkB))
                        st_[h] = (qkst, qkB, qs, qsB, sigs)
                    lfs = {}
                    for h in pair:
                        for d in range(2):
                            sig, sigB, kk, kkB = st_[h][4][d]
                            lf, lfB = lf_ring.next()
                            act(lf[:, 0:ntok], sig[:, 0:ntok], AF.Ln, [sigB, lbB], [lfB], scale=omlT[:, h:h + 1], bias=lbT[:, h:h + 1])
                            A, AB = A_ring.next()
                            if d == 0:
                                P.op("dve", lambda e, o=A[:, 0:ntok], m=smask[:, 0:ntok], l=lf[:, 0:ntok]:
                                     e.tensor_tensor_scan(out=o, data0=m, data1=l, initial=0.0, op0=ALU.mult, op1=ALU.add),
                                     reads=[lfB, smaskB], writes=[AB])
                            else:
                                P.op("dve", lambda e, o=A[:, 0:ntok][:, ::-1], m=smask[:, 0:ntok], l=lf[:, 0:ntok][:, ::-1]:
                                     e.tensor_tensor_scan(out=o, data0=m, data1=l, initial=0.0, op0=ALU.mult, op1=ALU.add),
                                     reads=[lfB, smaskB], writes=[AB])
                            lfs[(h, d)] = (lf, lfB, A, AB)
                    for h in pair:
                        qkst, qkB, qs, qsB, sigs = st_[h]
                        for d in range(2):
                            sig, sigB, kk, kkB = sigs[d]
                            lf, lfB, A, AB = lfs[(h, d)]
                            eA, eAB = lf, lfB
                            act(eA[:, 0:ntok], A[:, 0:ntok], AF.Exp, [AB], [eAB])
                            act(A[:, 0:ntok], A[:, 0:ntok], AF.Exp, [AB], [AB], scale=-1.0)
                            stt(qkst[:, d, 0, 0:ntok], qs[:, 0:ntok], 128.0 ** -0.5, eA[:, 0:ntok], ALU.mult, ALU.mult, [qsB, eAB], [qkB])
                            tt("pool", qkst[:, d, 1, 0:ntok], kk[:, 0:ntok], A[:, 0:ntok], ALU.mult, [kkB, AB], [qkB])
                            nch = ntok // 64
                            c0 = tiles[0] * 2
                            e3 = eA[:, 0:ntok].rearrange("p (c t) -> p c t", t=64)
                            srcgl = e3[:, :, 63] if d == 0 else e3[:, :, 0]
                            cp("pool", GLt[:, d, h, c0:c0 + nch], srcgl, [eAB], [GLB])
                        for ti, t in enumerate(tiles):
                            if stop_after == "hg1_noqk":
                                break
                            for d in range(2):
                                dst = QKs[b, t].rearrange("p (d h a n) -> p d h a n", d=2, h=8, a=2)[:, d, h, :, :]
                                dma("sp", dst, qkst[:, d, :, ti * 128:(ti + 1) * 128], [qkB], [dbuf("QK", b, t)])
            AR.release(m_gl)
            P.barrier()
            if stop_after in ("hg1", "hg1_v", "hg1_noqk"):
                AR.release(m0)
                return

            m1 = AR.mark()
            Vh = AR.alloc([NT, 1024], BF); VhB = [Buf("Vh%d" % t) for t in range(NT)]
            for t in range(NT):
                dma("sp", Vh[:, t, :], Vs[b, t * 128:(t + 1) * 128, :], [dbuf("V", b, t)], [VhB[t]])
            S = AR.alloc([16, 128], F32); Sbf = AR.alloc([16, 128], BF)
            SB = [Buf("S%d" % i) for i in range(16)]; SbfB = [Buf("Sbf%d" % i) for i in range(16)]
            for ci in range(16):
                memset("pool", S[:, ci, :], 0.0, [SB[ci]])
                memset("pool", Sbf[:, ci, :], 0.0, [SbfB[ci]])
            qk_ring = Ring([(AR.alloc([8, 2, 128], BF), Buf("qk%d" % i)) for i in range(4)])
            scm_ring = Ring([(AR.alloc([128], BF), Buf("scm%d" % i)) for i in range(8)])
            ktok_ring = Ring([(AR.alloc([128], BF), Buf("ktok%d" % i)) for i in range(8)])
            ost_ring = Ring([(AR.alloc([8, 128], F32), Buf("ost%d" % i)) for i in range(2)])
            sc_slots = Ring([(pbanks[i][:, 0:128], pbB[i]) for i in (0, 1)])
            tr_slots = Ring([(pbanks[i][:].bitcast(BF)[:, 0:128], pbB[i]) for i in (2, 7)])
            st_slots = Ring([(pbanks[i][:, 0:128], pbB[i]) for i in (3, 4)])
            prev_chunk = {}
            for rnd in range(NT):
                for d in range(2):
                    if rnd < 2:
                        t = (32 + rnd) if d == 0 else (33 - rnd)
                    else:
                        t = (rnd - 2) if d == 0 else (31 - (rnd - 2))
                    lat = t < 32
                    qk, qkB = qk_ring.next()
                    src = QKs[b, t][:, d * 2048:(d + 1) * 2048]
                    dma("sp", qk.rearrange("p h a n -> p (h a n)"), src, [dbuf("QK", b, t)], [qkB])
                    halves = [(0, 64), (64, 128)] if d == 0 else [(64, 128), (0, 64)]
                    obk = [pbanks[5], pbanks[6]]; obB = [pbB[5], pbB[6]]
                    stage = []
                    for h in range(8):
                        ci = d * 8 + h
                        scm = scmB = None
                        if lat:
                            scp, scpB = sc_slots.next()
                            mm(scp, qk[:, h, 1, :], qk[:, h, 0, :], True, True, [qkB], [scpB])
                            scm, scmB = scm_ring.next()
                            tt("dve", scm, scp, cmask[:, d, :], ALU.mult, [scpB, cmaskB], [scmB])
                        trp, trpB = tr_slots.next()
                        tr(trp, qk[:, h, 1, :], identb, [qkB, identbB], [trpB])
                        ktok, ktokB = ktok_ring.next()
                        cp("act", ktok, trp, [], [ktokB, trpB])
                        stage.append((ci, scm, scmB, ktok, ktokB))
                    for hi, (lo, hi_) in enumerate(halves):
                        for h in range(8):
                            ci, scm, scmB, ktok, ktokB = stage[h]
                            if lat:
                                ob = obk[h // 4]; oB = obB[h // 4]
                                oc = (h % 4) * 128
                                mm(ob[:, oc + lo:oc + hi_], Vh[:, t, h * 128:(h + 1) * 128], scm[:, lo:hi_], True, False, [VhB[t], scmB], [oB])
                                mm(ob[:, oc + lo:oc + hi_], Sbf[:, ci, :], qk[:, h, 0, lo:hi_], False, True, [SbfB[ci], qkB], [oB])
                            stp, stpB = st_slots.next()
                            mm(stp, ktok[lo:hi_, :], Vh[lo:hi_, t, h * 128:(h + 1) * 128], True, True, [ktokB, VhB[t]], [stpB])
                            chunk = t * 2 + (lo // 64)
                            gp = prev_chunk.get(ci, chunk)
                            prev_chunk[ci] = chunk
                            stt(S[:, ci, :], S[:, ci, :], GLt[:, d, h, gp:gp + 1], stp, ALU.mult, ALU.add, [SB[ci], GLB, stpB], [SB[ci]])
                            ts("pool", Sbf[:, ci, :], S[:, ci, :], GLt[:, d, h, chunk:chunk + 1], 1.0, ALU.mult, ALU.mult, [SB[ci], GLB], [SbfB[ci]])
                    if lat:
                        ost, ostB = ost_ring.next()
                        for hb in range(2):
                            cp("act", ost[:, hb * 4:(hb + 1) * 4, :].rearrange("p a b -> p (a b)"), obk[hb], [], [ostB, obB[hb]])
                        dma("act", Os[b, d, :, :, t * 128:(t + 1) * 128].rearrange("h p n -> p h n"), ost, [ostB], [dbuf("O", b, d, t)])
            AR.release(m1)
            P.barrier()
            if stop_after == "hg2":
                AR.release(m0)
                return

            m1 = AR.mark()
            o_ring = Ring([(AR.alloc([2, 512], F32), Buf("oin%d" % i)) for i in range(8)])
            sgi_ring = Ring([(AR.alloc([512], BF), Buf("sgin%d" % i)) for i in range(8)])
            sq_ring = Ring([(AR.alloc([512], BF), Buf("sq%d" % i)) for i in range(4)])
            rs_ring = Ring([(AR.alloc([512], F32), Buf("rs%d" % i)) for i in range(4)])
            gh_ring = Ring([(AR.alloc([8, 512], BF), Buf("gh%d" % i)) for i in range(2)])
            ss_ring = Ring([(pbanks[i], pbB[i]) for i in (0, 1, 6, 7)])
            ghs = {}

            def emit_block(blk):
                gh, ghB = gh_ring.next()
                recs = {}

                def stage1(h):
                    oin, oinB = o_ring.next()
                    for d in range(2):
                        dma("sp", oin[:, d, :], Os[b, d, h, :, blk * 512:(blk + 1) * 512],
                            [dbuf("O", b, d, blk * 4 + i) for i in range(4)], [oinB])
                    sgi, sgiB = sgi_ring.next()
                    dma("sp", sgi, SGs[b, h, :, blk * 512:(blk + 1) * 512], [dbuf("SG", b, h, blk)], [sgiB])
                    tt("pool", oin[:, 0, :], oin[:, 0, :], oin[:, 1, :], ALU.add, [oinB], [oinB])
                    sq, sqB = sq_ring.next()
                    act(sq, oin[:, 0, :], AF.Square, [oinB], [sqB])
                    ssp, sspB = ss_ring.next()
                    mm(ssp, onesb, sq, True, True, [onesB, sqB], [sspB])
                    recs[h] = (oin, oinB, sgi, sgiB, ssp, sspB)

                def stage2(h):
                    oin, oinB, sgi, sgiB, ssp, sspB = recs[h]
                    rs, rsB = rs_ring.next()
                    act(rs, ssp, AF.Ln, [epsB], [rsB, sspB], scale=1.0 / 128, bias=epsb[:, 0:1])
                    act(rs, rs, AF.Exp, [rsB], [rsB], scale=-0.5)
                    tt("dve", oin[:, 1, :], oin[:, 0, :], rs, ALU.mult, [oinB, rsB], [oinB])
                    stt(gh[:, h, :], oin[:, 1, :], hgnw[:, 0:1], sgi, ALU.mult, ALU.mult, [oinB, hgnwB, sgiB], [ghB])
                DEPTH = 3
                for h in range(8 + DEPTH):
                    if h < 8:
                        stage1(h)
                    if h - DEPTH >= 0:
                        stage2(h - DEPTH)
                return gh, ghB
            cur = {}

            def ghT(t):
                blk = t // 4
                for bb in (blk, blk + 1):
                    if bb < 8 and bb not in cur:
                        cur[bb] = emit_block(bb)
                cur.pop(blk - 1, None)
                gh, ghB = cur[blk]
                return gh, ghB, (t % 4) * 128
            out_proj(1, b, "hgout", None, list(range(32)), final=True, ghT=ghT)
            AR.release(m1)
            AR.release(m0)
            P.barrier()

        finals = []
        for b in range(2):
            na_layer(b)
            if stop_after == "na" and b == 0:
                pass
        if stop_after not in ("na", "na1", "na2"):
            for b in range(2):
                hg_layer(b)
        if dbg:
            finals += [v for k, v in DB.items() if k[0] == "Xs"]
        finals += [v for k, v in DB.items() if k[0] == "out"]
        nsem = P.emit(nc, finals)
        print("ops", P.n_ops, "sems", nsem, {e: len(P.ops[e]) for e in ENGS})
    return nc


def _consts():
    ident = np.eye(128, dtype=np.float32)
    ss = np.zeros((62, 64, 2, 64), np.float32)
    for qc in range(64):
        for kc in range(64):
            j = kc - qc + 15
            if 0 <= j <= 30:
                ss[j, qc, 0, kc] = 1.0
                ss[31 + j, qc, 1, kc] = 1.0
    nm = np.zeros((128, 64), np.float32)
    for qc in range(64):
        q0 = min(max(qc - 8, 0), 48)
        for kc in range(q0, q0 + 16):
            nm[kc, qc] = 1.0
            nm[64 + kc, qc] = 1.0
    cm = np.zeros((128, 2, 128), np.float32)
    for s in range(128):
        for t in range(128):
            if s // 64 == t // 64:
                if s <= t:
                    cm[s, 0, t] = 1.0
                if s >= t:
                    cm[s, 1, t] = 1.0
    sm = np.ones((128, 512), np.float32)
    sm[:, 0::64] = 0.0
    return {"k_ident": ident, "k_ss": ss.reshape(62, 8192), "k_nmask": nm, "k_cmask": cm.reshape(128, 256), "k_smask": sm}


_NC_CACHE = {}


def kernel(x, c, ctx, c_ctx, ada_w, ada_b, norm_w, na_w_in, na_rpb, na_w_out,
           hg_w_in, hg_lower, hg_norm_w, hg_w_out, final_norm_w, _dbg=False, _stop=None, _cores=8):
    f = lambda a: np.ascontiguousarray(np.asarray(a, dtype=np.float32))
    x, c, ctx = f(x), f(c), f(ctx)
    shared = {
        "c_ctx": f(c_ctx).reshape(1, 1024), "ada_w": f(ada_w), "ada_b": f(ada_b), "norm_w": f(norm_w),
        "na_w_in": f(na_w_in).reshape(1024, 4096), "na_rpb": f(na_rpb).reshape(240, 31),
        "na_w_out": f(na_w_out).reshape(1024, 1024), "hg_w_in": f(hg_w_in).reshape(1024, 5120),
        "hg_lower": f(hg_lower), "hg_norm_w": f(hg_norm_w).reshape(128, 1), "hg_w_out": f(hg_w_out).reshape(1024, 1024),
        "final_norm_w": f(final_norm_w).reshape(1, 1024),
    }
    shared.update(_consts())
    key = (_dbg, _stop)
    if key not in _NC_CACHE:
        _NC_CACHE[key] = build(dbg=_dbg, stop_after=_stop)
    nc = _NC_CACHE[key]
    in_maps = []
    for i in range(_cores):
        m = dict(shared)
        m["x"] = x[2 * i:2 * i + 2]
        m["c"] = c[2 * i:2 * i + 2]
        m["ctx"] = ctx[2 * i:2 * i + 2]
        in_maps.append(m)
    res = run_bass_kernel_spmd(nc, in_maps, core_ids=list(range(_cores)))
    if _dbg:
        return res
    return np.concatenate([r["out"] for r in res.results], axis=0)
```

```python
import contextlib
import numpy as np
import concourse.bass as bass
import concourse.mybir as mybir
from concourse.bass_utils import run_bass_kernel_spmd

F32 = mybir.dt.float32
BF = mybir.dt.bfloat16
AF = mybir.ActivationFunctionType
ALU = mybir.AluOpType

ENGS = ("pe", "act", "dve", "pool", "sp")
COMPUTE = ("pe", "act", "dve", "pool")

NT = 34
TALL = NT * 128
EPS = 1e-6


class Buf:
    __slots__ = ("name", "w", "r")

    def __init__(self, name=""):
        self.name = name
        self.w = None
        self.r = []


class Op:
    __slots__ = ("issuer", "tl", "tidx", "fn", "waits", "signal", "semval", "front", "is_dma")


class Prog:
    NDMA = 32
    EPOCH = 30000

    def __init__(self):
        self.ops = {e: [] for e in ENGS}
        self.tl_last = {}
        self.tl_count = {}
        self.known = {e: {} for e in ENGS}
        self.dma_rr = {"sp": 0, "pool": 0, "act": 0}
        self.barrier_ops = []
        self.barrier_pending = {}
        self.n_ops = 0

    @staticmethod
    def _merge(dst, src):
        for k, v in src.items():
            if dst.get(k, 0) < v:
                dst[k] = v

    def op(self, eng, fn, reads=(), writes=(), dma=False):
        o = Op()
        o.issuer = eng
        o.fn = fn
        o.signal = False
        o.semval = None
        o.is_dma = dma
        o.waits = []
        deps = []
        if dma:
            q = self.dma_rr[eng]
            self.dma_rr[eng] = q + 1
            o.tl = ("dma", eng, q % self.NDMA)
            prev = self.tl_last.get(o.tl)
            if prev is not None:
                deps.append(prev)
        else:
            o.tl = eng
        for b in reads:
            if b.w is not None:
                deps.append(b.w)
        for b in writes:
            if b.w is not None:
                deps.append(b.w)
            deps.extend(b.r)
        known = self.known[eng]
        if self.barrier_pending.get(eng):
            self.barrier_pending[eng] = False
            deps.extend(self.barrier_ops)
        deps.sort(key=lambda d: -d.tidx)
        for d in deps:
            if (not dma) and d.tl == eng:
                if eng == "pe":
                    continue
                pass
            if known.get(d.tl, 0) >= d.tidx:
                continue
            o.waits.append(d)
            d.signal = True
            self._merge(known, d.front)
        self.tl_count[o.tl] = self.tl_count.get(o.tl, 0) + 1
        o.tidx = self.tl_count[o.tl]
        front = {k: v for k, v in known.items() if k in COMPUTE}
        prev = self.tl_last.get(o.tl)
        if prev is not None:
            self._merge(front, prev.front)
        front[o.tl] = o.tidx
        o.front = front
        self.tl_last[o.tl] = o
        if dma:
            o.signal = True
        for b in reads:
            b.r.append(o)
        for b in writes:
            b.w = o
            b.r = []
        self.ops[eng].append(o)
        self.n_ops += 1
        return o

    def barrier(self):
        self.barrier_ops = list(self.tl_last.values())
        self.barrier_pending = {e: True for e in ENGS}

    def emit(self, nc, final_bufs):
        self.op("sp", None, reads=final_bufs)
        sems = {}
        cnt = {}
        stack = contextlib.ExitStack()
        with stack:
            def get_sem(key):
                if key not in sems:
                    nm = "s_" + "_".join(str(k) for k in (key if isinstance(key, tuple) else (key,)))
                    sems[key] = stack.enter_context(nc.semaphore(nm))
                return sems[key]
            for e in ENGS:
                for o in self.ops[e]:
                    if not o.signal:
                        continue
                    c = cnt.get(o.tl, 0) + 1
                    cnt[o.tl] = c
                    if o.is_dma:
                        o.semval = (get_sem(o.tl), 16 * c)
                    else:
                        ep = (c - 1) // self.EPOCH
                        o.semval = (get_sem((o.tl, ep)), (c - 1) % self.EPOCH + 1)
            block = stack.enter_context(nc.Block())
            engmap = {"pe": block.tensor, "act": block.scalar, "dve": block.vector,
                      "pool": block.gpsimd, "sp": block.sync}
            for e in ENGS:
                ops = self.ops[e]

                def body(eng, ops=ops):
                    for o in ops:
                        for d in o.waits:
                            s, v = d.semval
                            eng.wait_ge(s, v)
                        if o.fn is None:
                            continue
                        ins = o.fn(eng)
                        if o.signal:
                            s, v = o.semval
                            ins.then_inc(s, 16 if o.is_dma else 1)
                if ops:
                    engmap[e](body)
        return len(sems)


class Arena:
    def __init__(self, t, nwords):
        self.t = t
        self.n = nwords
        self.off = 0

    def alloc(self, free_shape, dt):
        nel = 1
        for s in free_shape:
            nel *= s
        esz = 4 if dt == F32 else 2
        words = (nel * esz + 3) // 4
        words = (words + 15) // 16 * 16
        assert self.off + words <= self.n, ("arena overflow", self.off, words, self.n)
        a = self.t[:, self.off:self.off + words]
        self.off += words
        if dt != F32:
            a = a.bitcast(dt)
        a = a[:, 0:nel]
        if len(free_shape) == 2:
            a = a.rearrange("p (a b) -> p a b", a=free_shape[0])
        elif len(free_shape) == 3:
            a = a.rearrange("p (a b c) -> p a b c", a=free_shape[0], b=free_shape[1])
        elif len(free_shape) == 4:
            a = a.rearrange("p (a b c d) -> p a b c d", a=free_shape[0], b=free_shape[1], c=free_shape[2])
        return a

    def mark(self):
        return self.off

    def release(self, m):
        self.off = m


class Ring:
    def __init__(self, items):
        self.items = items
        self.i = 0

    def next(self):
        it = self.items[self.i % len(self.items)]
        self.i += 1
        return it


def build(dbg=False, stop_after=None):
    nc = bass.Bass("TRN2", target_bir_lowering=False)
    P = Prog()

    def din(name, shape, dt=F32):
        return nc.dram_tensor(name, list(shape), dt, kind="ExternalInput").ap()

    x_d = din("x", [2, 4096, 1024])
    ctx_d = din("ctx", [2, 256, 1024])
    c_d = din("c", [2, 1024])
    cctx_d = din("c_ctx", [1, 1024])
    adaw_d = din("ada_w", [2, 1024, 3072])
    adab_d = din("ada_b", [2, 3072])
    normw_d = din("norm_w", [2, 1024])
    nawin_d = din("na_w_in", [1024, 4096])
    rpb_d = din("na_rpb", [240, 31])
    nawout_d = din("na_w_out", [1024, 1024])
    hgwin_d = din("hg_w_in", [1024, 5120])
    hglow_d = din("hg_lower", [2, 1024])
    hgnw_d = din("hg_norm_w", [128, 1])
    hgwout_d = din("hg_w_out", [1024, 1024])
    fnw_d = din("final_norm_w", [1, 1024])
    ident_d = din("k_ident", [128, 128])
    ss_d = din("k_ss", [62, 8192])
    nmask_d = din("k_nmask", [128, 64])
    cmask_d = din("k_cmask", [128, 256])
    smask_d = din("k_smask", [128, 512])
    out_d = nc.dram_tensor("out", [2, 4096, 1024], F32, kind="ExternalOutput").ap()

    def dscr(name, shape, dt, out=False):
        if out:
            return nc.dram_tensor(name, list(shape), dt, kind="ExternalOutput").ap()
        return nc.dram_tensor(name, list(shape), dt).ap()

    Xs = dscr("Xs", [2, TALL, 1024], F32, out=dbg)
    Gate = dscr("Gate", [2, 3, 1024], F32)
    QTs = dscr("QTs", [2, NT, 128, 1024], BF)
    KTs = dscr("KTs", [2, NT, 128, 1024], BF)
    VAs = dscr("VAs", [2, TALL, 1024], BF)
    ZTs = dscr("ZTs", [2, NT, 128, 1024], BF)
    QKs = dscr("QKs", [2, NT, 128, 4096], BF)
    SGs = dscr("SGs", [2, 8, 128, 4096], BF)
    Vs = dscr("Vs", [2, TALL, 1024], BF)
    Os = dscr("Os", [2, 2, 8, 128, 4096], F32)
    Es = dscr("Es", [128, 16 * 14 * 64], BF)
    WB = {"nain": dscr("Wb_nain", [1024, 4096], BF), "naout": dscr("Wb_naout", [1024, 1024], BF),
          "hgin": dscr("Wb_hgin", [1024, 5120], BF), "hgout": dscr("Wb_hgout", [1024, 1024], BF)}
    DB = {}
    def dbuf(*key):
        if key not in DB:
            DB[key] = Buf(str(key))
        return DB[key]

    st = contextlib.ExitStack()
    with st:
        AW = 51200
        arena_t = st.enter_context(nc.sbuf_tensor("arena", [128, AW], F32))
        AR = Arena(arena_t, AW)
        pbanks = [st.enter_context(nc.psum_tensor("pb%d" % i, [128, 512], F32))[:] for i in range(8)]
        pbB = [Buf("pb%d" % i) for i in range(8)]

        def dma(eng, out, in_, reads, writes):
            return P.op(eng, lambda e, out=out, in_=in_: e.dma_start(out=out, in_=in_), reads=reads, writes=writes, dma=True)

        def mm(out, lhsT, rhs, start, stop, reads, writes):
            return P.op("pe", lambda e, out=out, lhsT=lhsT, rhs=rhs, start=start, stop=stop:
                        e.matmul(out, lhsT=lhsT, rhs=rhs, start=start, stop=stop), reads=reads, writes=writes)

        def tr(out, in_, ident, reads, writes):
            return P.op("pe", lambda e, out=out, in_=in_, ident=ident: e.transpose(out=out, in_=in_, identity=ident),
                        reads=reads, writes=writes)

        def act(out, in_, func, reads, writes, scale=1.0, bias=0.0, accum=None):
            def f(e, out=out, in_=in_, func=func, scale=scale, bias=bias, accum=accum):
                kw = {}
                if accum is not None:
                    kw["accum_out"] = accum
                return e.activation(out=out, in_=in_, func=func, bias=bias, scale=scale, **kw)
            return P.op("act", f, reads=reads, writes=writes)

        def tt(eng, out, in0, in1, op, reads, writes):
            return P.op(eng, lambda e, out=out, in0=in0, in1=in1, op=op: e.tensor_tensor(out=out, in0=in0, in1=in1, op=op),
                        reads=reads, writes=writes)

        def ts(eng, out, in0, s1, s2, op0, op1, reads, writes):
            if s2 is None:
                return P.op(eng, lambda e, out=out, in0=in0, s1=s1, op0=op0:
                            e.tensor_scalar(out=out, in0=in0, scalar1=s1, scalar2=None, op0=op0), reads=reads, writes=writes)
            return P.op(eng, lambda e, out=out, in0=in0, s1=s1, s2=s2, op0=op0, op1=op1:
                        e.tensor_scalar(out=out, in0=in0, scalar1=s1, scalar2=s2, op0=op0, op1=op1), reads=reads, writes=writes)

        def stt(out, in0, scalar, in1, op0, op1, reads, writes):
            return P.op("dve", lambda e, out=out, in0=in0, scalar=scalar, in1=in1, op0=op0, op1=op1:
                        e.scalar_tensor_tensor(out=out, in0=in0, scalar=scalar, in1=in1, op0=op0, op1=op1),
                        reads=reads, writes=writes)

        def cp(eng, out, in_, reads, writes):
            if eng == "act":
                return P.op("act", lambda e, out=out, in_=in_: e.activation(out=out, in_=in_, func=AF.Copy), reads=reads, writes=writes)
            return P.op(eng, lambda e, out=out, in_=in_: e.tensor_copy(out=out, in_=in_), reads=reads, writes=writes)

        def memset(eng, ap, val, writes):
            return P.op(eng, lambda e, ap=ap, val=val: e.memset(ap, val), writes=writes)

        ident = AR.alloc([128], F32); identB = Buf("ident")
        identb = AR.alloc([128], BF); identbB = Buf("identb")
        ones = AR.alloc([128], F32); onesB = Buf("ones")
        onesb = AR.alloc([128], BF)
        R0T = AR.alloc([8, 8], F32); R0TB = Buf("R0T")
        scT = AR.alloc([8, 3], F32); scTB = Buf("scT")
        lbT = AR.alloc([8], F32); omlT = AR.alloc([8], F32); nomlT = AR.alloc([8], F32); lbB = Buf("lb")
        MOD = [AR.alloc([24, 3], F32) for _ in range(2)]; MODB = [Buf("mod0"), Buf("mod1")]
        A1 = [AR.alloc([8, 3], F32) for _ in range(2)]; A1B = [Buf("a10"), Buf("a11")]
        hgnw = AR.alloc([1], F32); hgnwB = Buf("hgnw")
        cmask = AR.alloc([2, 128], F32); cmaskB = Buf("cmask")
        smask = AR.alloc([512], F32); smaskB = Buf("smask")
        epsb = AR.alloc([1], F32); epsB = Buf("eps")
        dma("sp", ident, ident_d, [], [identB])
        dma("sp", hgnw, hgnw_d, [], [hgnwB])
        dma("sp", cmask.rearrange("p a b -> p (a b)"), cmask_d, [], [cmaskB])
        dma("sp", smask, smask_d, [], [smaskB])
        cp("dve", identb, ident, [identB], [identbB])
        memset("pool", ones, 1.0, [onesB])
        memset("pool", onesb, 1.0, [onesB])
        memset("pool", epsb, EPS, [epsB])

        persist_mark = AR.mark()

        stg_ring = Ring([(AR.alloc([1024], F32), Buf("stg%d" % i)) for i in range(6)])
        stb_ring = Ring([(AR.alloc([1024], BF), Buf("stb%d" % i)) for i in range(6)])
        ci_ = 0
        for name, src_d, ncols in (("nain", nawin_d, 4096), ("naout", nawout_d, 1024), ("hgin", hgwin_d, 5120), ("hgout", hgwout_d, 1024)):
            for c0 in range(0, ncols, 1024):
                for k in range(8):
                    stg, stgB = stg_ring.next()
                    stb, stbB = stb_ring.next()
                    dma("sp", stg, src_d[k * 128:(k + 1) * 128, c0:c0 + 1024], [], [stgB])
                    cp(("act", "dve", "pool")[ci_ % 3], stb, stg, [stgB], [stbB])
                    ci_ += 1
                    dma("act", WB[name][k * 128:(k + 1) * 128, c0:c0 + 1024], stb, [stbB], [dbuf("Wb", name, k, c0 // 1024)])

        R0 = AR.alloc([1024], F32); R0B = Buf("R0")
        memset("pool", R0[0:8, :], 0.0, [R0B])
        dma("sp", R0[0:2, :], c_d, [], [R0B])
        dma("sp", R0[2:3, :], cctx_d, [], [R0B])
        dma("sp", R0[3:5, :], normw_d, [], [R0B])
        dma("sp", R0[5:7, :], hglow_d, [], [R0B])
        tp = pbanks[0]
        for k in range(8):
            tr(tp[:, k * 8:(k + 1) * 8], R0[0:8, k * 128:(k + 1) * 128], ident[0:8, 0:8], [R0B, identB], [pbB[0]])
        cp("dve", R0T.rearrange("p a b -> p (a b)"), tp[:, 0:64], [pbB[0]], [R0TB])
        act(scT, R0T[:, :, 0:3], AF.Silu, [R0TB], [scTB])
        tt("dve", lbT, R0T[:, :, 6], R0T[:, :, 5], ALU.subtract, [R0TB], [lbB])
        act(lbT, lbT, AF.Sigmoid, [lbB], [lbB])
        ts("dve", omlT, lbT, -1.0, 1.0, ALU.mult, ALU.add, [lbB], [lbB])
        ts("dve", nomlT, omlT, -1.0, None, ALU.mult, None, [lbB], [lbB])

        Mrow = AR.alloc([3072], F32); MrowB = Buf("Mrow")
        adab = AR.alloc([3072], F32); adabB = Buf("adab")
        awr = Ring([(AR.alloc([8, 512], F32), Buf("aw%d" % i)) for i in range(2)])
        for i in range(2):
            dma("sp", adab[0:3, :], adab_d[i:i + 1, :].partition_broadcast(3), [], [adabB])
            for cb in range(6):
                aw, awB = awr.next()
                dma("sp", aw, adaw_d[i, :, cb * 512:(cb + 1) * 512].rearrange("(k p) n -> p k n", p=128), [], [awB])
                mp = pbanks[1 + (cb % 2)]; mpB = pbB[1 + (cb % 2)]
                for k in range(8):
                    mm(mp[0:3, :], scT[:, k, :], aw[:, k, :], k == 0, k == 7, [scTB, awB], [mpB])
                tt("dve", Mrow[0:3, cb * 512:(cb + 1) * 512], mp[0:3, :], adab[0:3, cb * 512:(cb + 1) * 512], ALU.add,
                   [mpB, adabB], [MrowB])
            dma("sp", Gate[i], Mrow[0:3, 2048:3072], [MrowB], [dbuf("gate", i)])
            tp = pbanks[3]
            for ch in range(24):
                tr(tp[:, ch * 3:(ch + 1) * 3], Mrow[0:3, ch * 128:(ch + 1) * 128], ident[0:3, 0:3], [MrowB, identB], [pbB[3]])
            cp("dve", MOD[i].rearrange("p a b -> p (a b)"), tp[:, 0:72], [pbB[3]], [MODB[i]])
            ts("dve", A1[i], MOD[i][:, 8:16, :], 1.0, None, ALU.add, None, [MODB[i]], [A1B[i]])
            nb = bass.AP(R0T.tensor, R0T[:, 0, 3 + i].offset, [[R0T.ap[0][0], 128], [8, 8], [0, 3]])
            tt("dve", A1[i], A1[i], nb, ALU.mult, [A1B[i], R0TB], [A1B[i]])
        AR.release(persist_mark)
        P.barrier()

        def token_src(layer, b, t):
            if layer == 0:
                if t < 32:
                    return x_d[b, t * 128:(t + 1) * 128, :], None
                return ctx_d[b, (t - 32) * 128:(t - 31) * 128, :], None
            return Xs[b, t * 128:(t + 1) * 128, :], dbuf("Xs", b, t)

        def make_hT(layer, b, tiles, xin_ring, hT_ring, junk, stat_ring):
            j = b if tiles[0] < 32 else 2
            ntok = 128 * len(tiles)
            xs = []
            for t in tiles:
                xin, xinB = xin_ring.next()
                src, sB = token_src(layer, b, t)
                dma("sp", xin, src, [sB] if sB else [], [xinB])
                stt_, stB = stat_ring.next()
                act(junk[0], xin, AF.Square, [xinB], [junk[1], stB], accum=stt_[:, 0:1])
                ts("dve", stt_[:, 1:2], stt_[:, 0:1], 1.0 / 1024, EPS, ALU.mult, ALU.add, [stB], [stB])
                act(stt_[:, 2:3], stt_[:, 1:2], AF.Ln, [stB], [stB])
                act(stt_[:, 3:4], stt_[:, 2:3], AF.Exp, [stB], [stB], scale=-0.5)
                ts("pool", xin, xin, stt_[:, 3:4], 1.0, ALU.mult, ALU.mult, [xinB, stB], [xinB])
                xs.append((xin, xinB))
            hT, hTB = hT_ring.next()
            for k in range(8):
                tpb = k % 2
                tp = pbanks[tpb]
                for ti, (xin, xinB) in enumerate(xs):
                    tr(tp[:, ti * 128:(ti + 1) * 128], xin[:, k * 128:(k + 1) * 128], ident, [xinB, identB], [pbB[tpb]])
                if k % 2 == 0:
                    act(hT[:, k, 0:ntok], tp[:, 0:ntok], AF.Identity, [pbB[tpb], A1B[layer], MODB[layer]], [hTB],
                        scale=A1[layer][:, k, j:j + 1], bias=MOD[layer][:, k, j:j + 1])
                else:
                    ts("dve", hT[:, k, 0:ntok], tp[:, 0:ntok], A1[layer][:, k, j:j + 1], MOD[layer][:, k, j:j + 1],
                       ALU.mult, ALU.add, [pbB[tpb], A1B[layer], MODB[layer]], [hTB])
            return hT, hTB

        class WBufs:
            def __init__(self, n):
                self.b = [Buf("w%d" % i) for i in range(n)]

        def load_w_bf16(dst, name, ncols):
            wb = WBufs(ncols // 512)
            src = WB[name]
            for cb in range(ncols // 512):
                dma("sp", dst[:, :, cb * 512:(cb + 1) * 512], src[:, cb * 512:(cb + 1) * 512].rearrange("(k p) n -> p k n", p=128),
                    [dbuf("Wb", name, k, cb // 2) for k in range(8)], [wb.b[cb]])
            return wb

        BLOCKS = [list(range(i * 4, i * 4 + 4)) for i in range(8)] + [[32, 33]]

        def na_layer(b):
            m0 = AR.mark()
            Win = AR.alloc([8, 4096], BF)
            WinW = load_w_bf16(Win, "nain", 4096)
            xin_ring = Ring([(AR.alloc([1024], F32), Buf("xin%d" % i)) for i in range(6)])
            hT_ring = Ring([(AR.alloc([8, 512], BF), Buf("hT%d" % i)) for i in range(2)])
            junk = (AR.alloc([1024], BF), Buf("junk"))
            stat_ring = Ring([(AR.alloc([4], F32), Buf("st%d" % i)) for i in range(8)])
            qst_ring = Ring([(AR.alloc([8, 512], BF), Buf("qst%d" % i)) for i in range(2)])
            kst_ring = Ring([(AR.alloc([8, 512], BF), Buf("kst%d" % i)) for i in range(2)])
            vst_ring = Ring([(AR.alloc([4, 1024], BF), Buf("vst%d" % i)) for i in range(2)])
            zst_ring = Ring([(AR.alloc([8, 512], BF), Buf("zst%d" % i)) for i in range(2)])
            pring = Ring([(pbanks[i], pbB[i]) for i in range(2, 8)])
            ev = 0
            for tiles in BLOCKS:
                ntok = 128 * len(tiles)
                hT, hTB = make_hT(0, b, tiles, xin_ring, hT_ring, junk, stat_ring)
                qst, qstB = qst_ring.next()
                kst, kstB = kst_ring.next()
                vst, vstB = vst_ring.next()
                zst, zstB = zst_ring.next()
                for jc in list(range(16)) + list(range(24, 32)):
                    pp, ppB = pring.next()
                    for k in range(8):
                        mm(pp[:, 0:ntok], Win[:, k, jc * 128:(jc + 1) * 128], hT[:, k, 0:ntok], k == 0, k == 7, [WinW.b[jc // 4], hTB], [ppB])
                    if jc >= 24:
                        act(zst[:, jc - 24, 0:ntok], pp[:, 0:ntok], AF.Silu, [ppB], [zstB])
                        continue
                    dst, dB = (qst, qstB) if jc < 8 else (kst, kstB)
                    eng = "act" if ev % 2 == 0 else "dve"
                    ev += 1
                    cp(eng, dst[:, jc % 8, 0:ntok], pp[:, 0:ntok], [ppB], [dB])
                for ti, t in enumerate(tiles):
                    dma("act", QTs[b, t].rearrange("p (c n) -> p c n", c=8), qst[:, :, ti * 128:(ti + 1) * 128], [qstB], [dbuf("QT", b, t)])
                    dma("act", KTs[b, t].rearrange("p (c n) -> p c n", c=8), kst[:, :, ti * 128:(ti + 1) * 128], [kstB], [dbuf("KT", b, t)])
                    dma("act", ZTs[b, t].rearrange("p (c n) -> p c n", c=8), zst[:, :, ti * 128:(ti + 1) * 128], [zstB], [dbuf("ZT", b, t)])
                for ti, t in enumerate(tiles):
                    for half in range(2):
                        pp, ppB = pring.next()
                        c0 = 2048 + half * 512
                        for k in range(8):
                            mm(pp, hT[:, k, ti * 128:(ti + 1) * 128], Win[:, k, c0:c0 + 512], k == 0, k == 7, [WinW.b[c0 // 512], hTB], [ppB])
                        eng = "dve" if ev % 2 == 0 else "act"
                        ev += 1
                        cp(eng, vst[:, ti, half * 512:(half + 1) * 512], pp, [ppB], [vstB])
                    dma("act", VAs[b, t * 128:(t + 1) * 128, :], vst[:, ti, :], [vstB], [dbuf("VA", b, t)])
            AR.release(m0)
            P.barrier()
            if stop_after == "na1":
                return

            m0 = AR.mark()
            E = AR.alloc([16 * 14, 64], BF); EB = Buf("E")
            if b == 0:
                m1 = AR.mark()
                X2 = AR.alloc([2, 62], F32); X2B = Buf("X2")
                R2 = AR.alloc([240], F32); R2B = Buf("R2")
                SS = AR.alloc([64, 128], F32); SSB = Buf("SS")
                nmask = AR.alloc([64], F32); nmaskB = Buf("nmask")
                Eraw = AR.alloc([240, 64], F32); ErawB = Buf("Eraw")
                memset("pool", X2.rearrange("p a b -> p (a b)"), 0.0, [X2B])
                for tI in range(2):
                    dma("sp", X2[0:120, tI, 0:31], rpb_d[tI * 120:(tI + 1) * 120, :], [], [X2B])
                    n2 = 120 if tI == 0 else 119
                    dma("sp", X2[0:n2, tI, 31:62], rpb_d[tI * 120 + 1:tI * 120 + 1 + n2, :], [], [X2B])
                dma("sp", SS[0:62, :, :].rearrange("p a b -> p (a b)"), ss_d, [], [SSB])
                dma("sp", nmask, nmask_d, [], [nmaskB])
                for tI in range(2):
                    tr(pbanks[0][0:62, tI * 120:(tI + 1) * 120], X2[0:120, tI, :], ident[0:120, 0:120], [X2B, identB], [pbB[0]])
                cp("dve", R2[0:62, :], pbanks[0][0:62, 0:240], [pbB[0]], [R2B])
                for qc in range(64):
                    bk = qc % 2
                    mm(pbanks[bk][:, 0:240], SS[0:62, qc, :], R2[0:62, :], True, True, [SSB, R2B], [pbB[bk]])
                    act(Eraw[:, :, qc], pbanks[bk][:, 0:240], AF.Exp, [pbB[bk]], [ErawB])
                for h in range(16):
                    nmb = bass.AP(nmask.tensor, nmask.offset, [[nmask.ap[0][0], 128], [0, 14], [1, 64]])
                    er = bass.AP(Eraw.tensor, Eraw[:, h * 15 + 13, :].offset, [[Eraw.ap[0][0], 128], [-64, 14], [1, 64]])
                    tt("dve" if h % 2 else "pool", E[:, h * 14:(h + 1) * 14, :], er, nmb, ALU.mult,
                       [ErawB, nmaskB], [EB])
                AR.release(m1)

                dma("sp", Es, E.rearrange("p a b -> p (a b)"), [EB], [dbuf("Es")])
            else:
                dma("sp", E.rearrange("p a b -> p (a b)"), Es, [dbuf("Es")], [EB])
            P.barrier()

            Wo = AR.alloc([8, 1024], BF)
            WoW = load_w_bf16(Wo, "naout", 1024)
            gate = {}
            gateB = Buf("gatebc")
            for j in (b, 2):
                gate[j] = AR.alloc([1024], F32)
                dma("sp", gate[j], Gate[0, j:j + 1, :].partition_broadcast(128), [dbuf("gate", 0)], [gateB])
            NKS = 12
            ktr = [(AR.alloc([8, 128], BF), Buf("kt%d" % i)) for i in range(NKS)]
            v2r = [(AR.alloc([16, 128], BF), Buf("v2%d" % i)) for i in range(NKS)]
            ktc = [(AR.alloc([8, 128], BF), Buf("ktc%d" % i)) for i in range(2)]
            v2c = [(AR.alloc([16, 128], BF), Buf("v2c%d" % i)) for i in range(2)]
            for v2, vB in v2r + v2c:
                v4 = v2.rearrange("p (h2 two) e -> p h2 two e", two=2)
                memset("pool", v4[:, :, 0, 64:128], 1.0, [vB])
                memset("pool", v4[:, :, 1, 0:64], 1.0, [vB])
            q_ring = Ring([(AR.alloc([8, 2, 512], BF), Buf("q%d" % i)) for i in range(1)])
            for qg_, qB_ in q_ring.items:
                memset("pool", qg_.rearrange("p a b c -> p (a b c)"), 0.0, [qB_])
            z_ring = Ring([(AR.alloc([8, 512], BF), Buf("z%d" % i)) for i in range(1)])
            gt_ring = Ring([(AR.alloc([8, 512], BF), Buf("gt%d" % i)) for i in range(1)])
            pt_ring = Ring([(AR.alloc([512], BF), Buf("pt%d" % i)) for i in range(9)])
            rc_ring = Ring([(AR.alloc([512], F32), Buf("rc%d" % i)) for i in range(1)])
            tm_ring = Ring([(AR.alloc([512], F32), Buf("tm%d" % i)) for i in range(1)])
            xo_ring = Ring([(AR.alloc([1024], F32), Buf("xo%d" % i)) for i in range(2)])
            yt_ring = Ring([(AR.alloc([512], F32), Buf("yt%d" % i)) for i in range(2)])
            s_ring = Ring([(pbanks[i], pbB[i]) for i in range(5)])
            obank = [(pbanks[5], pbB[5]), (pbanks[6], pbB[6])]
            y_ring = Ring([(pbanks[i], pbB[i]) for i in (7,)])
            LOOK = 8
            loaded = {}

            def key_tile(a):
                if a in loaded:
                    return loaded[a]
                if a >= 32:
                    kt, kB = ktc[a - 32]
                    v2, vB = v2c[a - 32]
                else:
                    kt, kB = ktr[a % NKS]
                    v2, vB = v2r[a % NKS]
                dma("sp", kt, KTs[b, a].rearrange("p (c n) -> p c n", c=8), [dbuf("KT", b, a)], [kB])
                src = VAs[b, a * 128:(a + 1) * 128, :].rearrange("p (h2 two d) -> p h2 two d", two=2, d=64)
                v4 = v2.rearrange("p (h2 two) e -> p h2 two e", two=2)
                dma("sp", v4[:, :, 0, 0:64], src[:, :, 0, :], [dbuf("VA", b, a)], [vB])
                dma("sp", v4[:, :, 1, 64:128], src[:, :, 1, :], [dbuf("VA", b, a)], [vB])
                for old in [x for x in loaded if x < 32 and (x % NKS) == (a % NKS)]:
                    del loaded[old]
                loaded[a] = (kt, kB, v2, vB)
                return loaded[a]

            def r0_of(r):
                return min(max(r - 4, 0), 56)

            def group_tiles(g):
                if g == 8:
                    return []
                res_ = []
                for a in range(32):
                    full, up, lo = [], [], []
                    for r in range(8 * g, 8 * g + 8):
                        r0 = r0_of(r)
                        n0 = r0 <= 2 * a <= r0 + 7
                        n1 = r0 <= 2 * a + 1 <= r0 + 7
                        if n0 and n1:
                            full.append(r)
                        elif n1:
                            up.append(r)
                        elif n0:
                            lo.append(r)
                    allr = sorted(full + up + lo)
                    if not allr:
                        continue
                    assert allr == list(range(allr[0], allr[-1] + 1))
                    segs = []
                    for rows, (plo, phi) in ((full, (0, 128)), (up, (64, 128)), (lo, (0, 64))):
                        if rows:
                            assert rows == list(range(rows[0], rows[-1] + 1))
                            segs.append((plo, phi, rows[0], rows[-1]))
                    res_.append((a, allr[0], allr[-1], segs))
                return res_

            key_tile(32); key_tile(33)
            qcur = None
            for g in range(9):
                qtiles = list(range(4 * g, 4 * g + 4)) if g < 8 else [32, 33]
                n = 128 * len(qtiles)
                jg = b if g < 8 else 2
                rbase = 8 * g
                for gg in (g, g + 1):
                    if gg < 8:
                        for (a, _, _, _) in group_tiles(gg):
                            key_tile(a)
                qg, qB = q_ring.next()
                zg, zB = z_ring.next()
                for ti, t in enumerate(qtiles):
                    qsrc = QTs[b, t].rearrange("p (c n) -> p c n", c=8)
                    dma("sp", qg[0:64, :, 0, ti * 128:(ti + 1) * 128], qsrc[0:64], [dbuf("QT", b, t)], [qB])
                    dma("sp", qg[64:128, :, 1, ti * 128:(ti + 1) * 128], qsrc[64:128], [dbuf("QT", b, t)], [qB])
                for ti, t in enumerate(qtiles):
                    dma("sp", zg[:, :, ti * 128:(ti + 1) * 128], ZTs[b, t].rearrange("p (c n) -> p c n", c=8), [dbuf("ZT", b, t)], [zB])
                gt, gtB = gt_ring.next()
                gtiles = group_tiles(g)
                items = []
                for (a, u_lo, u_hi, segs) in gtiles:
                    items.append((a, (u_lo - rbase) * 64, (u_hi - rbase + 1) * 64,
                                  [(plo, phi, (r_lo - rbase) * 64, (r_hi - rbase + 1) * 64) for (plo, phi, r_lo, r_hi) in segs],
                                  2 * a - u_lo + 7, u_hi - u_lo + 1))
                for a in (32, 33):
                    items.append((a, 0, n, [(0, 128, 0, n)], None, None))
                nit = len(items)
                pend = []

                def emit_pv(rec):
                    h, idx, (a, c0, c1, segs, dlo, nU), pt, ptB, v2, vB = rec
                    ob, oB = obank[h % 2]
                    P.op("pe", lambda e, out=ob[:, c0:c1], lhsT=v2[:, h, :], rhs=pt[:, 0:c1 - c0], st_=(idx == 0), sp_=(idx == nit - 1):
                         e.matmul(out, lhsT=lhsT, rhs=rhs, start=st_, stop=sp_, skip_group_check=True), reads=[vB, ptB], writes=[oB])
                    if idx == nit - 1:
                        c, hf = h // 2, (h % 2) * 64
                        dh = 64 - hf
                        rc, rcB = rc_ring.next()
                        tm, tmB = tm_ring.next()
                        act(rc[hf:hf + 64, 0:n], ob[dh:dh + 64, 0:n], AF.Ln, [], [rcB, oB])
                        act(rc[hf:hf + 64, 0:n], rc[hf:hf + 64, 0:n], AF.Exp, [rcB], [rcB], scale=-1.0)
                        tt("dve", tm[hf:hf + 64, 0:n], ob[hf:hf + 64, 0:n], rc[hf:hf + 64, 0:n], ALU.mult, [rcB], [tmB, oB])
                        tt("pool", gt[hf:hf + 64, c, 0:n], tm[hf:hf + 64, 0:n], zg[hf:hf + 64, c, 0:n], ALU.mult, [tmB, zB], [gtB])
                for h in range(16):
                    c = h // 2
                    hf_ = (h % 2) * 64
                    for idx, it in enumerate(items):
                        (a, c0, c1, segs, dlo, nU) = it
                        kt, kB, v2, vB = key_tile(a)
                        sb_, sbB = s_ring.next()
                        w = c1 - c0
                        mm(sb_[:, 0:w], kt[:, c, :], qg[:, c, h % 2, c0:c1], True, True, [kB, qB], [sbB])
                        pt, ptB = pt_ring.next()
                        act(pt[:, 0:w], sb_[:, 0:w], AF.Exp, [], [ptB, sbB], scale=0.125)
                        if dlo is not None:
                            assert 0 <= dlo - (nU - 1) and dlo <= 13
                            e0 = h * 14 + 13 - dlo
                            e_ap = E[:, e0:e0 + nU, :].rearrange("p r q -> p (r q)")
                            em_eng = "pool" if (idx % 3 == 1) else "dve"
                            tt(em_eng, pt[:, 0:w], pt[:, 0:w], e_ap, ALU.mult, [ptB, EB], [ptB])
                            for (plo, phi, s0, s1) in segs:
                                if (plo, phi) == (64, 128):
                                    memset(em_eng, pt[0:64, s0 - c0:s1 - c0], 0.0, [ptB])
                                elif (plo, phi) == (0, 64):
                                    memset(em_eng, pt[64:128, s0 - c0:s1 - c0], 0.0, [ptB])
                        pend.append((h, idx, it, pt, ptB, v2, vB))
                        if len(pend) > LOOK:
                            emit_pv(pend.pop(0))
                while pend:
                    emit_pv(pend.pop(0))
                for ti, t in enumerate(qtiles):
                    xo, xoB = xo_ring.next()
                    src, sB = token_src(0, b, t)
                    dma("sp", xo, src, [], [xoB])
                    for half in range(2):
                        yt, ytB = yt_ring.next()
                        yp, ypB = y_ring.next()
                        for k in range(8):
                            mm(yp, gt[:, k, ti * 128:(ti + 1) * 128], Wo[:, k, half * 512:(half + 1) * 512], k == 0, k == 7, [gtB, WoW.b[half]], [ypB])
                        tt("dve", yt, yp, gate[jg][:, half * 512:(half + 1) * 512], ALU.mult, [gateB], [ytB, ypB])
                        tt("pool", xo[:, half * 512:(half + 1) * 512], xo[:, half * 512:(half + 1) * 512], yt, ALU.add, [xoB, ytB], [xoB])
                    dma("pool", Xs[b, t * 128:(t + 1) * 128, :], xo, [xoB], [dbuf("Xs", b, t)])
            AR.release(m0)
            P.barrier()

        def out_proj(layer, b, wout_d, gsrc, tiles, final, ghT=None):
            Wo = AR.alloc([8, 1024], BF)
            WoW = load_w_bf16(Wo, wout_d, 1024)
            gate = [None] * 3
            gateB = Buf("gatebc")
            for j in ([b, 2] if not final else [b]):
                gate[j] = AR.alloc([1024], F32)
                dma("sp", gate[j], Gate[layer, j:j + 1, :].partition_broadcast(128), [dbuf("gate", layer)], [gateB])
            if final:
                fnw = AR.alloc([1024], F32)
                dma("sp", fnw, fnw_d[0:1, :].partition_broadcast(128), [], [gateB])
            gin_ring = Ring([(AR.alloc([1024], BF), Buf("gin%d" % i)) for i in range(2)])
            gT_ring = Ring([(AR.alloc([8, 128], BF), Buf("gT%d" % i)) for i in range(2)])
            xo_ring = Ring([(AR.alloc([1024], F32), Buf("xo%d" % i)) for i in range(3)])
            tmp_ring = Ring([(AR.alloc([1024], F32), Buf("tmp%d" % i)) for i in range(2)])
            st_ring = Ring([(AR.alloc([4], F32), Buf("fst%d" % i)) for i in range(4)])
            junk = (AR.alloc([1024], BF), Buf("junk2"))
            y_ring = Ring([(pbanks[i], pbB[i]) for i in range(2, 6)])
            tp_ring = Ring([(pbanks[i], pbB[i]) for i in range(2)])
            for t in tiles:
                j = b if t < 32 else 2
                if ghT is None:
                    gin, ginB = gin_ring.next()
                    src, sBs = gsrc(t)
                    dma("sp", gin, src, sBs, [ginB])
                    tpp, tppB = tp_ring.next()
                    tpb = tpp[:].bitcast(BF)
                    for k in range(8):
                        tr(tpb[:, k * 128:(k + 1) * 128], gin[:, k * 128:(k + 1) * 128], identb, [ginB, identbB], [tppB])
                    gT, gTB = gT_ring.next()
                    cp("act", gT.rearrange("p a b -> p (a b)"), tpb, [tppB], [gTB])
                    lhs = lambda k: gT[:, k, :]
                else:
                    gfull, gTB, tcol = ghT(t)
                    lhs = lambda k: gfull[:, k, tcol:tcol + 128]
                xo, xoB = xo_ring.next()
                src, sB = token_src(layer, b, t)
                dma("sp", xo, src, [sB] if sB else [], [xoB])
                tmp, tmpB = tmp_ring.next()
                for half in range(2):
                    yp, ypB = y_ring.next()
                    for k in range(8):
                        mm(yp, lhs(k), Wo[:, k, half * 512:(half + 1) * 512], k == 0, k == 7, [gTB, WoW.b[half]], [ypB])
                    tt("dve", tmp[:, half * 512:(half + 1) * 512], yp, gate[j][:, half * 512:(half + 1) * 512], ALU.mult,
                       [ypB, gateB], [tmpB])
                tt("pool", xo, xo, tmp, ALU.add, [xoB, tmpB], [xoB])
                if not final:
                    dma("pool", Xs[b, t * 128:(t + 1) * 128, :], xo, [xoB], [dbuf("Xs", b, t)])
                else:
                    stt_, stB = st_ring.next()
                    act(junk[0], xo, AF.Square, [xoB], [junk[1], stB], accum=stt_[:, 0:1])
                    ts("dve", stt_[:, 1:2], stt_[:, 0:1], 1.0 / 1024, EPS, ALU.mult, ALU.add, [stB], [stB])
                    act(stt_[:, 2:3], stt_[:, 1:2], AF.Ln, [stB], [stB])
                    act(stt_[:, 3:4], stt_[:, 2:3], AF.Exp, [stB], [stB], scale=-0.5)
                    stt(tmp, xo, stt_[:, 3:4], fnw, ALU.mult, ALU.mult, [xoB, stB, gateB], [tmpB])
                    dma("pool", out_d[b, t * 128:(t + 1) * 128, :], tmp, [tmpB], [dbuf("out", b, t)])

        GL = None

        def hg_layer(b):
            m0 = AR.mark()
            GLt = AR.alloc([2, 8, 68], F32); GLB = Buf("GL")
            m_gl = AR.mark()
            Win = AR.alloc([8, 5120], BF)
            WinW = load_w_bf16(Win, "hgin", 5120)
            xin_ring = Ring([(AR.alloc([1024], F32), Buf("xin%d" % i)) for i in range(5)])
            hT_ring = Ring([(AR.alloc([8, 512], BF), Buf("hT%d" % i)) for i in range(2)])
            junk = (AR.alloc([1024], BF), Buf("junk"))
            stat_ring = Ring([(AR.alloc([4], F32), Buf("st%d" % i)) for i in range(8)])
            qk_ring = Ring([(AR.alloc([2, 2, 512], BF), Buf("qkst%d" % i)) for i in range(2)])
            sg_ring = Ring([(AR.alloc([512], BF), Buf("sgst%d" % i)) for i in range(2)])
            vst_ring = Ring([(AR.alloc([4, 1024], BF), Buf("vst%d" % i)) for i in range(2)])
            qs_ring = Ring([(AR.alloc([512], F32), Buf("qs%d" % i)) for i in range(3)])

            def tmpring(nm, cnt_):
                return Ring([(AR.alloc([512], F32), Buf("%s%d" % (nm, i))) for i in range(cnt_)])
            sgq_ring = tmpring("sgq", 2)
            sig_ring, lf_ring, kk_ring, A_ring = [tmpring(n, 4) for n in ("sig", "lf", "kk", "A")]
            pring = Ring([(pbanks[i], pbB[i]) for i in range(2, 8)])
            ev = 0
            for tiles in BLOCKS:
                nt_ = len(tiles)
                ntok = 128 * nt_
                hT, hTB = make_hT(1, b, tiles, xin_ring, hT_ring, junk, stat_ring)

                def proj_fm(col0):
                    pp, ppB = pring.next()
                    for k in range(8):
                        mm(pp[:, 0:ntok], Win[:, k, col0:col0 + 128], hT[:, k, 0:ntok], k == 0, k == 7, [WinW.b[col0 // 512], hTB], [ppB])
                    return pp, ppB
                vst, vstB = vst_ring.next()
                for ti, t in enumerate(tiles):
                    for half in range(2):
                        pp, ppB = pring.next()
                        c0 = 1024 + half * 512
                        for k in range(8):
                            mm(pp, hT[:, k, ti * 128:(ti + 1) * 128], Win[:, k, c0:c0 + 512], k == 0, k == 7, [WinW.b[c0 // 512], hTB], [ppB])
                        eng = "dve" if ev % 2 == 0 else "act"
                        ev += 1
                        cp(eng, vst[:, ti, half * 512:(half + 1) * 512], pp, [ppB], [vstB])
                    dma("sp", Vs[b, t * 128:(t + 1) * 128, :], vst[:, ti, :], [vstB], [dbuf("V", b, t)])
                for hp in range(4):
                    if stop_after == "hg1_v":
                        break
                    pair = (2 * hp, 2 * hp + 1)
                    st_ = {}
                    for h in pair:
                        qkst, qkB = qk_ring.next()
                        pp, ppB = proj_fm(h * 128)
                        sq_, sqB_ = sgq_ring.next()
                        act(sq_[:, 0:ntok], pp[:, 0:ntok], AF.Sigmoid, [], [sqB_, ppB])
                        qs, qsB = qs_ring.next()
                        tt("dve", qs[:, 0:ntok], pp[:, 0:ntok], sq_[:, 0:ntok], ALU.mult, [sqB_], [qsB, ppB])
                        pp, ppB = proj_fm(4096 + h * 128)
                        sq_, sqB_ = sgq_ring.next()
                        act(sq_[:, 0:ntok], pp[:, 0:ntok], AF.Sigmoid, [], [sqB_, ppB])
                        sg, sgB = sg_ring.next()
                        tt("dve", sg[:, 0:ntok], pp[:, 0:ntok], sq_[:, 0:ntok], ALU.mult, [sqB_], [sgB, ppB])
                        if tiles[0] < 32:
                            dma("sp", SGs[b, h, :, tiles[0] * 128:tiles[0] * 128 + ntok], sg[:, 0:ntok], [sgB], [dbuf("SG", b, h, tiles[0] // 4)])
                        sigs = []
                        for d in range(2):
                            pp, ppB = proj_fm(2048 + d * 1024 + h * 128)
                            sig, sigB = sig_ring.next()
                            act(sig[:, 0:ntok], pp[:, 0:ntok], AF.Sigmoid, [], [sigB, ppB])
                            kk, kkB = kk_ring.next()
                            ts("pool", kk[:, 0:ntok], sig[:, 0:ntok], nomlT[:, h:h + 1], omlT[:, h:h + 1], ALU.mult, ALU.add, [sigB, lbB], [kkB])
                            sigs.append((sig, sigB, kk, kkB))
                        st_[h] = (qkst, qkB, qs, qsB, sigs)
                    lfs = {}
                    for h in pair:
                        for d in range(2):
                            sig, sigB, kk, kkB = st_[h][4][d]
                            lf, lfB = lf_ring.next()
                            act(lf[:, 0:ntok], sig[:, 0:ntok], AF.Ln, [sigB, lbB], [lfB], scale=omlT[:, h:h + 1], bias=lbT[:, h:h + 1])
                            A, AB = A_ring.next()
                            if d == 0:
                                P.op("dve", lambda e, o=A[:, 0:ntok], m=smask[:, 0:ntok], l=lf[:, 0:ntok]:
                                     e.tensor_tensor_scan(out=o, data0=m, data1=l, initial=0.0, op0=ALU.mult, op1=ALU.add),
                                     reads=[lfB, smaskB], writes=[AB])
                            else:
                                P.op("dve", lambda e, o=A[:, 0:ntok][:, ::-1], m=smask[:, 0:ntok], l=lf[:, 0:ntok][:, ::-1]:
                                     e.tensor_tensor_scan(out=o, data0=m, data1=l, initial=0.0, op0=ALU.mult, op1=ALU.add),
                                     reads=[lfB, smaskB], writes=[AB])
                            lfs[(h, d)] = (lf, lfB, A, AB)
                    for h in pair:
                        qkst, qkB, qs, qsB, sigs = st_[h]
                        for d in range(2):
                            sig, sigB, kk, kkB = sigs[d]
                            lf, lfB, A, AB = lfs[(h, d)]
                            eA, eAB = lf, lfB
                            act(eA[:, 0:ntok], A[:, 0:ntok], AF.Exp, [AB], [eAB])
                            act(A[:, 0:ntok], A[:, 0:ntok], AF.Exp, [AB], [AB], scale=-1.0)
                            stt(qkst[:, d, 0, 0:ntok], qs[:, 0:ntok], 128.0 ** -0.5, eA[:, 0:ntok], ALU.mult, ALU.mult, [qsB, eAB], [qkB])
                            tt("pool", qkst[:, d, 1, 0:ntok], kk[:, 0:ntok], A[:, 0:ntok], ALU.mult, [kkB, AB], [qkB])
                            nch = ntok // 64
                            c0 = tiles[0] * 2
                            e3 = eA[:, 0:ntok].rearrange("p (c t) -> p c t", t=64)
                            srcgl = e3[:, :, 63] if d == 0 else e3[:, :, 0]
                            cp("pool", GLt[:, d, h, c0:c0 + nch], srcgl, [eAB], [GLB])
                        for ti, t in enumerate(tiles):
                            if stop_after == "hg1_noqk":
                                break
                            for d in range(2):
                                dst = QKs[b, t].rearrange("p (d h a n) -> p d h a n", d=2, h=8, a=2)[:, d, h, :, :]
                                dma("sp", dst, qkst[:, d, :, ti * 128:(ti + 1) * 128], [qkB], [dbuf("QK", b, t)])
            AR.release(m_gl)
            P.barrier()
            if stop_after in ("hg1", "hg1_v", "hg1_noqk"):
                AR.release(m0)
                return

            m1 = AR.mark()
            Vh = AR.alloc([NT, 1024], BF); VhB = [Buf("Vh%d" % t) for t in range(NT)]
            for t in range(NT):
                dma("sp", Vh[:, t, :], Vs[b, t * 128:(t + 1) * 128, :], [dbuf("V", b, t)], [VhB[t]])
            S = AR.alloc([16, 128], F32); Sbf = AR.alloc([16, 128], BF)
            SB = [Buf("S%d" % i) for i in range(16)]; SbfB = [Buf("Sbf%d" % i) for i in range(16)]
            for ci in range(16):
                memset("pool", S[:, ci, :], 0.0, [SB[ci]])
                memset("pool", Sbf[:, ci, :], 0.0, [SbfB[ci]])
            qk_ring = Ring([(AR.alloc([8, 2, 128], BF), Buf("qk%d" % i)) for i in range(4)])
            scm_ring = Ring([(AR.alloc([128], BF), Buf("scm%d" % i)) for i in range(8)])
            ktok_ring = Ring([(AR.alloc([128], BF), Buf("ktok%d" % i)) for i in range(8)])
            ost_ring = Ring([(AR.alloc([8, 128], F32), Buf("ost%d" % i)) for i in range(2)])
            sc_slots = Ring([(pbanks[i][:, 0:128], pbB[i]) for i in (0, 1)])
            tr_slots = Ring([(pbanks[i][:].bitcast(BF)[:, 0:128], pbB[i]) for i in (2, 7)])
            st_slots = Ring([(pbanks[i][:, 0:128], pbB[i]) for i in (3, 4)])
            prev_chunk = {}
            for rnd in range(NT):
                for d in range(2):
                    if rnd < 2:
                        t = (32 + rnd) if d == 0 else (33 - rnd)
                    else:
                        t = (rnd - 2) if d == 0 else (31 - (rnd - 2))
                    lat = t < 32
                    qk, qkB = qk_ring.next()
                    src = QKs[b, t][:, d * 2048:(d + 1) * 2048]
                    dma("sp", qk.rearrange("p h a n -> p (h a n)"), src, [dbuf("QK", b, t)], [qkB])
                    halves = [(0, 64), (64, 128)] if d == 0 else [(64, 128), (0, 64)]
                    obk = [pbanks[5], pbanks[6]]; obB = [pbB[5], pbB[6]]
                    stage = []
                    for h in range(8):
                        ci = d * 8 + h
                        scm = scmB = None
                        if lat:
                            scp, scpB = sc_slots.next()
                            mm(scp, qk[:, h, 1, :], qk[:, h, 0, :], True, True, [qkB], [scpB])
                            scm, scmB = scm_ring.next()
                            tt("dve", scm, scp, cmask[:, d, :], ALU.mult, [scpB, cmaskB], [scmB])
                        trp, trpB = tr_slots.next()
                        tr(trp, qk[:, h, 1, :], identb, [qkB, identbB], [trpB])
                        ktok, ktokB = ktok_ring.next()
                        cp("act", ktok, trp, [], [ktokB, trpB])
                        stage.append((ci, scm, scmB, ktok, ktokB))
                    for hi, (lo, hi_) in enumerate(halves):
                        for h in range(8):
                            ci, scm, scmB, ktok, ktokB = stage[h]
                            if lat:
                                ob = obk[h // 4]; oB = obB[h // 4]
                                oc = (h % 4) * 128
                                mm(ob[:, oc + lo:oc + hi_], Vh[:, t, h * 128:(h + 1) * 128], scm[:, lo:hi_], True, False, [VhB[t], scmB], [oB])
                                mm(ob[:, oc + lo:oc + hi_], Sbf[:, ci, :], qk[:, h, 0, lo:hi_], False, True, [SbfB[ci], qkB], [oB])
                            stp, stpB = st_slots.next()
                            mm(stp, ktok[lo:hi_, :], Vh[lo:hi_, t, h * 128:(h + 1) * 128], True, True, [ktokB, VhB[t]], [stpB])
                            chunk = t * 2 + (lo // 64)
                            gp = prev_chunk.get(ci, chunk)
                            prev_chunk[ci] = chunk
                            stt(S[:, ci, :], S[:, ci, :], GLt[:, d, h, gp:gp + 1], stp, ALU.mult, ALU.add, [SB[ci], GLB, stpB], [SB[ci]])
                            ts("pool", Sbf[:, ci, :], S[:, ci, :], GLt[:, d, h, chunk:chunk + 1], 1.0, ALU.mult, ALU.mult, [SB[ci], GLB], [SbfB[ci]])
                    if lat:
                        ost, ostB = ost_ring.next()
                        for hb in range(2):
                            cp("act", ost[:, hb * 4:(hb + 1) * 4, :].rearrange("p a b -> p (a b)"), obk[hb], [], [ostB, obB[hb]])
                        dma("act", Os[b, d, :, :, t * 128:(t + 1) * 128].rearrange("h p n -> p h n"), ost, [ostB], [dbuf("O", b, d, t)])
            AR.release(m1)
            P.barrier()
            if stop_after == "hg2":
                AR.release(m0)
                return

            m1 = AR.mark()
            o_ring = Ring([(AR.alloc([2, 512], F32), Buf("oin%d" % i)) for i in range(8)])
            sgi_ring = Ring([(AR.alloc([512], BF), Buf("sgin%d" % i)) for i in range(8)])
            sq_ring = Ring([(AR.alloc([512], BF), Buf("sq%d" % i)) for i in range(4)])
            rs_ring = Ring([(AR.alloc([512], F32), Buf("rs%d" % i)) for i in range(4)])
            gh_ring = Ring([(AR.alloc([8, 512], BF), Buf("gh%d" % i)) for i in range(2)])
            ss_ring = Ring([(pbanks[i], pbB[i]) for i in (0, 1, 6, 7)])
            ghs = {}

            def emit_block(blk):
                gh, ghB = gh_ring.next()
                recs = {}

                def stage1(h):
                    oin, oinB = o_ring.next()
                    for d in range(2):
                        dma("sp", oin[:, d, :], Os[b, d, h, :, blk * 512:(blk + 1) * 512],
                            [dbuf("O", b, d, blk * 4 + i) for i in range(4)], [oinB])
                    sgi, sgiB = sgi_ring.next()
                    dma("sp", sgi, SGs[b, h, :, blk * 512:(blk + 1) * 512], [dbuf("SG", b, h, blk)], [sgiB])
                    tt("pool", oin[:, 0, :], oin[:, 0, :], oin[:, 1, :], ALU.add, [oinB], [oinB])
                    sq, sqB = sq_ring.next()
                    act(sq, oin[:, 0, :], AF.Square, [oinB], [sqB])
                    ssp, sspB = ss_ring.next()
                    mm(ssp, onesb, sq, True, True, [onesB, sqB], [sspB])
                    recs[h] = (oin, oinB, sgi, sgiB, ssp, sspB)

                def stage2(h):
                    oin, oinB, sgi, sgiB, ssp, sspB = recs[h]
                    rs, rsB = rs_ring.next()
                    act(rs, ssp, AF.Ln, [epsB], [rsB, sspB], scale=1.0 / 128, bias=epsb[:, 0:1])
                    act(rs, rs, AF.Exp, [rsB], [rsB], scale=-0.5)
                    tt("dve", oin[:, 1, :], oin[:, 0, :], rs, ALU.mult, [oinB, rsB], [oinB])
                    stt(gh[:, h, :], oin[:, 1, :], hgnw[:, 0:1], sgi, ALU.mult, ALU.mult, [oinB, hgnwB, sgiB], [ghB])
                DEPTH = 3
                for h in range(8 + DEPTH):
                    if h < 8:
                        stage1(h)
                    if h - DEPTH >= 0:
                        stage2(h - DEPTH)
                return gh, ghB
            cur = {}

            def ghT(t):
                blk = t // 4
                for bb in (blk, blk + 1):
                    if bb < 8 and bb not in cur:
                        cur[bb] = emit_block(bb)
                cur.pop(blk - 1, None)
                gh, ghB = cur[blk]
                return gh, ghB, (t % 4) * 128
            out_proj(1, b, "hgout", None, list(range(32)), final=True, ghT=ghT)
            AR.release(m1)
            AR.release(m0)
            P.barrier()

        finals = []
        for b in range(2):
            na_layer(b)
            if stop_after == "na" and b == 0:
                pass
        if stop_after not in ("na", "na1", "na2"):
            for b in range(2):
                hg_layer(b)
        if dbg:
            finals += [v for k, v in DB.items() if k[0] == "Xs"]
        finals += [v for k, v in DB.items() if k[0] == "out"]
        nsem = P.emit(nc, finals)
        print("ops", P.n_ops, "sems", nsem, {e: len(P.ops[e]) for e in ENGS})
    return nc


def _consts():
    ident = np.eye(128, dtype=np.float32)
    ss = np.zeros((62, 64, 2, 64), np.float32)
    for qc in range(64):
        for kc in range(64):
            j = kc - qc + 15
            if 0 <= j <= 30:
                ss[j, qc, 0, kc] = 1.0
                ss[31 + j, qc, 1, kc] = 1.0
    nm = np.zeros((128, 64), np.float32)
    for qc in range(64):
        q0 = min(max(qc - 8, 0), 48)
        for kc in range(q0, q0 + 16):
            nm[kc, qc] = 1.0
            nm[64 + kc, qc] = 1.0
    cm = np.zeros((128, 2, 128), np.float32)
    for s in range(128):
        for t in range(128):
            if s // 64 == t // 64:
                if s <= t:
                    cm[s, 0, t] = 1.0
                if s >= t:
                    cm[s, 1, t] = 1.0
    sm = np.ones((128, 512), np.float32)
    sm[:, 0::64] = 0.0
    return {"k_ident": ident, "k_ss": ss.reshape(62, 8192), "k_nmask": nm, "k_cmask": cm.reshape(128, 256), "k_smask": sm}


_NC_CACHE = {}


def kernel(x, c, ctx, c_ctx, ada_w, ada_b, norm_w, na_w_in, na_rpb, na_w_out,
           hg_w_in, hg_lower, hg_norm_w, hg_w_out, final_norm_w, _dbg=False, _stop=None, _cores=8):
    f = lambda a: np.ascontiguousarray(np.asarray(a, dtype=np.float32))
    x, c, ctx = f(x), f(c), f(ctx)
    shared = {
        "c_ctx": f(c_ctx).reshape(1, 1024), "ada_w": f(ada_w), "ada_b": f(ada_b), "norm_w": f(norm_w),
        "na_w_in": f(na_w_in).reshape(1024, 4096), "na_rpb": f(na_rpb).reshape(240, 31),
        "na_w_out": f(na_w_out).reshape(1024, 1024), "hg_w_in": f(hg_w_in).reshape(1024, 5120),
        "hg_lower": f(hg_lower), "hg_norm_w": f(hg_norm_w).reshape(128, 1), "hg_w_out": f(hg_w_out).reshape(1024, 1024),
        "final_norm_w": f(final_norm_w).reshape(1, 1024),
    }
    shared.update(_consts())
    key = (_dbg, _stop)
    if key not in _NC_CACHE:
        _NC_CACHE[key] = build(dbg=_dbg, stop_after=_stop)
    nc = _NC_CACHE[key]
    in_maps = []
    for i in range(_cores):
        m = dict(shared)
        m["x"] = x[2 * i:2 * i + 2]
        m["c"] = c[2 * i:2 * i + 2]
        m["ctx"] = ctx[2 * i:2 * i + 2]
        in_maps.append(m)
    res = run_bass_kernel_spmd(nc, in_maps, core_ids=list(range(_cores)))
    if _dbg:
        return res
    return np.concatenate([r["out"] for r in res.results], axis=0)
```
